# Optimizing a Trainium2 kernel written in Bass

```python
import math
import jax, jax.numpy as jnp
from jax import lax
import numpy as np

D_MODEL = 1024
BATCH = 1
SEQ = 16384
DEPTH = 1

HEAD_DIM = 64
N_DIFF_HEADS = 4
DIFF_V_DIM = 2 * HEAD_DIM
N_SB_HEADS = 8
DIFF_QK_WIDTH = N_DIFF_HEADS * 2 * HEAD_DIM
DIFF_V_WIDTH = N_DIFF_HEADS * DIFF_V_DIM
SB_WIDTH = N_SB_HEADS * HEAD_DIM
MIX_WIDTH = DIFF_V_WIDTH + SB_WIDTH
QKV_WIDTH = 2 * DIFF_QK_WIDTH + DIFF_V_WIDTH + 3 * SB_WIDTH
QKV_SPLITS = (DIFF_QK_WIDTH, 2 * DIFF_QK_WIDTH, 2 * DIFF_QK_WIDTH + DIFF_V_WIDTH,
              2 * DIFF_QK_WIDTH + DIFF_V_WIDTH + SB_WIDTH,
              2 * DIFF_QK_WIDTH + DIFF_V_WIDTH + 2 * SB_WIDTH)
D_FF = 2816
CONV_WIDTH = 3
N_BUCKETS = 32
MAX_DISTANCE = 128
Q_BLOCK = 128
RMS_EPS = 1e-6

kernel_name = 'hybrid_diffattn_stickbreak_convffn_block'


def _rmsnorm(x, g):
    xf = x.astype(jnp.float32)
    y = xf * lax.rsqrt(jnp.mean(xf * xf, axis=-1, keepdims=True) + RMS_EPS)
    return (y * g.astype(jnp.float32)).astype(x.dtype)


def _lambda_init(layer):
    return 0.8 - 0.6 * math.exp(-0.3 * layer)


def _relative_bucket(rel):
    n = jnp.maximum(rel, 0)
    max_exact = N_BUCKETS // 2
    nf = jnp.maximum(n, 1).astype(jnp.float32)
    large = max_exact + (jnp.log(nf / max_exact) / math.log(MAX_DISTANCE / max_exact)
                         * (N_BUCKETS - max_exact)).astype(jnp.int32)
    large = jnp.minimum(large, N_BUCKETS - 1)
    return jnp.where(n < max_exact, n, large)


def _diff_attention(q, k, v, lam, rel_bias):
    B, H, _, S, d = q.shape
    nb = S // Q_BLOCK
    qb = q.reshape(B, H, 2, nb, Q_BLOCK, d).transpose(3, 0, 1, 2, 4, 5)
    starts = jnp.arange(nb, dtype=jnp.int32) * Q_BLOCK
    k_pos = jnp.arange(S, dtype=jnp.int32)
    scale = d ** -0.5
    table = rel_bias.astype(jnp.float32)

    def one_block(args):
        q_blk, t0 = args
        rel = (t0 + jnp.arange(Q_BLOCK, dtype=jnp.int32))[:, None] - k_pos[None, :]
        bias = jnp.transpose(table[_relative_bucket(rel)], (2, 0, 1))
        logits = jnp.einsum('bhmqd,bhmkd->bhmqk', q_blk, k).astype(jnp.float32) * scale
        logits = logits + bias[None, :, None]
        logits = jnp.where(rel >= 0, logits, -jnp.inf)
        p = jax.nn.softmax(logits, axis=-1)
        w = p[:, :, 0] - lam * p[:, :, 1]
        return jnp.einsum('bhqk,bhkd->bhqd', w.astype(v.dtype), v)

    out = lax.map(one_block, (qb, starts))
    return out.transpose(1, 2, 0, 3, 4).reshape(B, H, S, v.shape[-1])


def _stick_breaking_attention(q, k, v):
    B, H, S, d = q.shape
    nb = S // Q_BLOCK
    qb = q.reshape(B, H, nb, Q_BLOCK, d).transpose(2, 0, 1, 3, 4)
    starts = jnp.arange(nb, dtype=jnp.int32) * Q_BLOCK
    k_pos = jnp.arange(S, dtype=jnp.int32)
    scale = d ** -0.5

    def one_block(args):
        q_blk, t0 = args
        rel = (t0 + jnp.arange(Q_BLOCK, dtype=jnp.int32))[:, None] - k_pos[None, :]
        strict = rel > 0
        z = jnp.einsum('bhqd,bhkd->bhqk', q_blk, k).astype(jnp.float32) * scale
        log_beta = jax.nn.log_sigmoid(z)
        log_1mb = jnp.where(strict, jax.nn.log_sigmoid(-z), 0.0)
        between = lax.cumsum(log_1mb, axis=3, reverse=True) - log_1mb
        a = jnp.where(strict, jnp.exp(log_beta + between), 0.0)
        return jnp.einsum('bhqk,bhkd->bhqd', a.astype(v.dtype), v)

    out = lax.map(one_block, (qb, starts))
    return out.transpose(1, 2, 0, 3, 4).reshape(B, H, S, d)


def _causal_depthwise_conv(u, w, b):
    S = u.shape[1]
    up = jnp.pad(u, ((0, 0), (CONV_WIDTH - 1, 0), (0, 0)))
    y = b
    for i in range(CONV_WIDTH):
        y = y + up[:, i:i + S] * w[i]
    return y


def setup_inputs(seed: int = 0) -> dict:
    key = jax.random.key(seed)
    ks = jax.random.split(key, 18)

    def nrm(k, shape, scale):
        return jax.random.normal(k, shape, jnp.float32) * scale

    def gain(k, shape):
        return 1.0 + 0.05 * jax.random.normal(k, shape, jnp.float32)

    return {
        'x': nrm(ks[0], (BATCH, SEQ, D_MODEL), 1.0),
        'attn_pre_norm': gain(ks[1], (DEPTH, D_MODEL)),
        'w_qkv': nrm(ks[2], (DEPTH, D_MODEL, QKV_WIDTH), D_MODEL ** -0.5),
        'lambda_q1': nrm(ks[3], (DEPTH, HEAD_DIM), 0.1),
        'lambda_k1': nrm(ks[4], (DEPTH, HEAD_DIM), 0.1),
        'lambda_q2': nrm(ks[5], (DEPTH, HEAD_DIM), 0.1),
        'lambda_k2': nrm(ks[6], (DEPTH, HEAD_DIM), 0.1),
        'diff_subln': gain(ks[7], (DEPTH, DIFF_V_DIM)),
        'sb_norm': gain(ks[8], (DEPTH, HEAD_DIM)),
        'rel_bias': nrm(ks[9], (N_BUCKETS, N_DIFF_HEADS), 0.5),
        'w_o': nrm(ks[10], (DEPTH, MIX_WIDTH, D_MODEL), MIX_WIDTH ** -0.5),
        'attn_post_norm': gain(ks[11], (DEPTH, D_MODEL)),
        'ffn_pre_norm': gain(ks[12], (DEPTH, D_MODEL)),
        'w_up': nrm(ks[13], (DEPTH, D_MODEL, 2 * D_FF), D_MODEL ** -0.5),
        'conv_w': nrm(ks[14], (DEPTH, CONV_WIDTH, 2 * D_FF), CONV_WIDTH ** -0.5),
        'conv_b': nrm(ks[15], (DEPTH, 2 * D_FF), 0.01),
        'w_down': nrm(ks[16], (DEPTH, D_FF, D_MODEL), D_FF ** -0.5),
        'ffn_post_norm': gain(ks[17], (DEPTH, D_MODEL)),
    }


def reference(x, attn_pre_norm, w_qkv, lambda_q1, lambda_k1, lambda_q2, lambda_k2,
              diff_subln, sb_norm, rel_bias, w_o, attn_post_norm, ffn_pre_norm,
              w_up, conv_w, conv_b, w_down, ffn_post_norm):
    B, S, _ = x.shape
    for l in range(DEPTH):
        lam_init = _lambda_init(l)
        h = _rmsnorm(x, attn_pre_norm[l])
        qkv = h @ w_qkv[l]
        dq, dk, dv, sq, sk, sv = jnp.split(qkv, QKV_SPLITS, axis=-1)
        dq = dq.reshape(B, S, N_DIFF_HEADS, 2, HEAD_DIM).transpose(0, 2, 3, 1, 4)
        dk = dk.reshape(B, S, N_DIFF_HEADS, 2, HEAD_DIM).transpose(0, 2, 3, 1, 4)
        dv = dv.reshape(B, S, N_DIFF_HEADS, DIFF_V_DIM).transpose(0, 2, 1, 3)
        lam = (jnp.exp(jnp.sum(lambda_q1[l].astype(jnp.float32) * lambda_k1[l].astype(jnp.float32)))
               - jnp.exp(jnp.sum(lambda_q2[l].astype(jnp.float32) * lambda_k2[l].astype(jnp.float32)))
               + lam_init)
        diff_o = _diff_attention(dq, dk, dv, lam, rel_bias)
        diff_o = _rmsnorm(diff_o, diff_subln[l]) * (1.0 - lam_init)
        diff_o = diff_o.transpose(0, 2, 1, 3).reshape(B, S, DIFF_V_WIDTH)

        sq = sq.reshape(B, S, N_SB_HEADS, HEAD_DIM).transpose(0, 2, 1, 3)
        sk = sk.reshape(B, S, N_SB_HEADS, HEAD_DIM).transpose(0, 2, 1, 3)
        sv = sv.reshape(B, S, N_SB_HEADS, HEAD_DIM).transpose(0, 2, 1, 3)
        sb_o = _rmsnorm(_stick_breaking_attention(sq, sk, sv), sb_norm[l])
        sb_o = sb_o.transpose(0, 2, 1, 3).reshape(B, S, SB_WIDTH)

        mix = jnp.concatenate([diff_o, sb_o], axis=-1) @ w_o[l]
        x = x + _rmsnorm(mix, attn_post_norm[l])

        h = _rmsnorm(x, ffn_pre_norm[l])
        u = _causal_depthwise_conv(h @ w_up[l], conv_w[l], conv_b[l])
        gate, val = jnp.split(u, 2, axis=-1)
        y = (jax.nn.gelu(gate, approximate=True) * val) @ w_down[l]
        x = x + _rmsnorm(y, ffn_post_norm[l])
    return x
```

```python
import math
import numpy as np
import ml_dtypes
import concourse.bass as bass
import concourse.mybir as mybir
from concourse.bass_utils import run_bass_kernel_spmd

F32 = mybir.dt.float32
BF = mybir.dt.bfloat16
AF = mybir.ActivationFunctionType
ALU = mybir.AluOpType

NCORES = 8
D = 1024
HD = 64
EPS = 1e-6
LAM_INIT = 0.8 - 0.6 * math.exp(-0.3 * 0)
BIG = 32768.0
NEG = -30000.0
DFF = 2816
NJ = DFF // 128


class Cfg:
    def __init__(self, seq, nslot, w):
        self.SEQ = seq
        self.NSLOT = nslot
        self.W = w
        self.TC = w + 2
        self.NBLK = seq // 128
        self.NQ = nslot * self.TC
        assert 8 * nslot * w >= seq
        self.NB = [min(self.NBLK, -(-(8 * (m + 1) * w) // 128)) for m in range(nslot)]
        self.OFF0 = max(128 * (self.NB[m] - 1) - 8 * w * m for m in range(nslot)) + 160
        self.TABLEN = max(8 * w * m + self.OFF0 for m in range(nslot)) + self.TC + 256

    def t0(self, c, m):
        return self.W * (8 * m + c) - 2

    def sb_masked(self, m, kb):
        return (self.t0(0, m) - 1) - (128 * kb + 127) < 0

    def df_masked(self, m, kb):
        return self.t0(0, m) - (128 * kb + 127) < 113

    def base(self, m, kb):
        return 8 * self.W * m - 128 * kb + self.OFF0


FULL = Cfg(16384, 5, 410)


class Op:
    __slots__ = ("eng", "fn", "deps", "sig", "need", "dma", "idx", "tag")

    def __init__(self, eng, fn, dma=None, tag=""):
        self.eng = eng
        self.fn = fn
        self.deps = []
        self.sig = None
        self.need = False
        self.dma = dma
        self.tag = tag


class Prog:
    ENGS = ("pe", "act", "dve", "pool", "sp")

    def __init__(self, nc):
        self.nc = nc
        self.ops = {e: [] for e in self.ENGS}
        self.lastw = {}
        self.readers = {}
        self.dma_count = {}
        self.dma_depth = {}
        self.all_dma_since_barrier = []
        self.out_dmas = []

    def _add(self, op, reads, writes):
        deps = []
        for r in reads:
            w = self.lastw.get(r)
            if w is not None:
                deps.append(w)
        for w_ in writes:
            w = self.lastw.get(w_)
            if w is not None:
                deps.append(w)
            deps.extend(self.readers.get(w_, ()))
        seen = set()
        for d in deps:
            if d is op or id(d) in seen:
                continue
            seen.add(id(d))
            if d.eng == "pe" and op.eng == "pe" and d.dma is None and op.dma is None:
                continue
            op.deps.append(d)
            d.need = True
        for r in reads:
            self.readers.setdefault(r, []).append(op)
        for w_ in writes:
            self.lastw[w_] = op
            self.readers[w_] = []
        self.ops[op.eng].append(op)
        return op

    def op(self, eng, fn, reads=(), writes=(), tag=""):
        return self._add(Op(eng, fn, tag=tag), reads, writes)

    def dma(self, eng, stream, depth, fn, reads=(), writes=(), tag=""):
        n = self.dma_count.get(stream, 0)
        self.dma_count[stream] = n + 1
        self.dma_depth[stream] = depth
        o = Op(eng, fn, dma=(stream, n % depth, 16 * (n // depth + 1)), tag=tag)
        o.need = True
        self.all_dma_since_barrier.append(o)
        return self._add(o, reads, writes)

    def barrier(self):
        lasts = [self.ops[e][-1] for e in self.ENGS if self.ops[e]]
        dmas = list(self.all_dma_since_barrier)
        self.all_dma_since_barrier = []
        for e in self.ENGS:
            o = Op(e, None, tag="barrier")
            for d in lasts + dmas:
                if d.eng == e and d.dma is None:
                    continue
                o.deps.append(d)
                d.need = True
            self.ops[e].append(o)
        self.lastw = {}
        self.readers = {}

    def emit(self):
        nc = self.nc
        import contextlib
        with contextlib.ExitStack() as st:
            esem = {e: st.enter_context(nc.semaphore("sem_" + e)) for e in ("pe", "act", "dve", "pool")}
            dsem = {}
            for s, dep in self.dma_depth.items():
                for i in range(dep):
                    dsem[(s, i)] = st.enter_context(nc.semaphore("d_%s_%d" % (s, i)))
            cnt = {e: 0 for e in esem}
            for e in self.ENGS:
                for o in self.ops[e]:
                    if o.dma is not None:
                        o.sig = (dsem[(o.dma[0], o.dma[1])], o.dma[2])
                    elif o.need and o.fn is not None:
                        cnt[e] += 1
                        o.sig = (esem[e], cnt[e])
                    elif o.need and o.fn is None:
                        o.sig = None
            block = st.enter_context(nc.Block())
            prog = self

            def run(eng_name, eng):
                waited = {}
                for o in prog.ops[eng_name]:
                    need = {}
                    for d in o.deps:
                        if d.sig is None:
                            continue
                        sem, val = d.sig
                        k = id(sem)
                        if k not in need or need[k][1] < val:
                            need[k] = (sem, val)
                    for k, (sem, val) in need.items():
                        if waited.get(k, 0) >= val:
                            continue
                        waited[k] = val
                        eng.wait_ge(sem, val)
                    if o.fn is None:
                        continue
                    ins = o.fn(eng)
                    if o.sig is not None:
                        if o.dma is not None:
                            ins.then_inc(o.sig[0], 16)
                        else:
                            ins.then_inc(o.sig[0], 1)
                if eng_name == "sp":
                    for o in prog.out_dmas:
                        sem, val = o.sig
                        if waited.get(id(sem), 0) < val:
                            waited[id(sem)] = val
                            eng.wait_ge(sem, val)

            @block.tensor
            def _(e):
                run("pe", e)

            @block.scalar
            def _(e):
                run("act", e)

            @block.vector
            def _(e):
                run("dve", e)

            @block.gpsimd
            def _(e):
                run("pool", e)

            @block.sync
            def _(e):
                run("sp", e)


class Arena:
    def __init__(self, ap_f32, nbytes):
        self.ap = ap_f32
        self.n = nbytes
        self.off = 0
        self.marks = []

    def alloc(self, nbytes, dtype=F32):
        assert nbytes % 4 == 0
        rounded = (nbytes + 31) // 32 * 32
        assert self.off + rounded <= self.n, ("arena overflow", self.off, rounded, self.n)
        a = self.ap[:, self.off // 4:(self.off + nbytes) // 4]
        self.off += rounded
        if dtype == BF:
            a = a.bitcast(BF)
        return a

    def mark(self):
        return self.off

    def reset(self, m):
        self.off = m


def v3(ap, a):
    return ap.rearrange("p (a b) -> p a b", a=a)


def build(cfg, debug=None):
    SEQ, NSLOT, W, TC, NBLK, NQ = cfg.SEQ, cfg.NSLOT, cfg.W, cfg.TC, cfg.NBLK, cfg.NQ
    nc = bass.Bass("TRN2", target_bir_lowering=False)

    def din(name, shape, dt=F32):
        return nc.dram_tensor(name, list(shape), dt, kind="ExternalInput").ap()

    x_rev = din("x_rev", [SEQ, D])
    xq = din("xq", [NQ, D])
    w_qkv = din("w_qkv", [D, 3072])
    w_o = din("w_o", [D, D])
    w_up = din("w_up", [D, 2 * DFF])
    w_down = din("w_down", [DFF, D])
    gpre = din("gpre", [128, 8])
    gpost = din("gpost", [128, 8])
    gffn = din("gffn", [128, 8])
    gpost2 = din("gpost2", [128, 8])
    cw_in = din("cw", [128, 44 * 3])
    cb_in = din("cb", [128, 44])
    lamv = din("lamv", [128, 4 * 64])
    gsub = din("gsub", [128, 1])
    gsb = din("gsb", [128, 1])
    b31 = din("b31", [128, 4])
    hv_in = din("hv", [128, NSLOT])
    posrow_in = din("posrow", [128, NQ])
    kcol_in = din("kcol", [128, NBLK])
    gtab = din("gtab", [4, cfg.TABLEN])
    consts_in = din("cmats", [128, 5 * 128], BF)
    identf_in = din("identf", [128, 128])
    out = nc.dram_tensor("out", [NQ, D], F32, kind="ExternalOutput").ap()
    if debug:
        dbg = nc.dram_tensor("dbg", [128, 8 * NQ], BF, kind="ExternalOutput").ap()

    kt_scr = nc.dram_tensor("kt_scr", [D, SEQ], BF).ap()
    vd_scr = nc.dram_tensor("vd_scr", [4, 128, NBLK * 128], BF).ap()
    vs_scr = nc.dram_tensor("vs_scr", [8, 128, NBLK * 128], BF).ap()

    import contextlib
    st = contextlib.ExitStack()
    ARENA_BYTES = 204 * 1024
    arena_t = st.enter_context(nc.sbuf_tensor("arena", [128, ARENA_BYTES // 4], F32))
    AR = Arena(arena_t[:, :], ARENA_BYTES)
    banks = [st.enter_context(nc.psum_tensor("bank%d" % i, [128, 512], F32)) for i in range(8)]
    PSF = [b[:, :] for b in banks]
    PSB = [b[:, :].bitcast(BF) for b in banks]

    P = Prog(nc)

    cm = AR.alloc(5 * 128 * 2, BF)
    IDENT = cm[:, 0:128]
    NEGTRI = cm[:, 128:256]
    NEGONES = cm[:, 256:384]
    ONES = cm[:, 384:512]
    ONESBLK = cm[:, 512:640]
    IDENTF = AR.alloc(128 * 4)
    g_pre = AR.alloc(8 * 4)
    g_post = AR.alloc(8 * 4)
    g_ffn = AR.alloc(8 * 4)
    g_post2 = AR.alloc(8 * 4)
    cw = AR.alloc(44 * 3 * 4)
    cb = AR.alloc(44 * 4)
    lam_t = AR.alloc(4 * 64 * 4)
    g_sub = AR.alloc(32)
    g_sb = AR.alloc(32)
    b31_t = AR.alloc(32)
    hv = AR.alloc(max(32, NSLOT * 4))
    neglam = AR.alloc(32)
    lamtmp = AR.alloc(4 * 64 * 4)
    lamred = AR.alloc(32)
    gsub8 = AR.alloc(32)
    kcol = AR.alloc(NBLK * 4)
    eps_t = AR.alloc(32)
    one_t = AR.alloc(32)
    P.op("pool", lambda e: e.memset(eps_t[:, 0:1], EPS), writes=["eps"])
    P.op("pool", lambda e: e.memset(one_t[:, 0:1], 1.0), writes=["one"])
    mix_region = AR.alloc(8 * NQ * 2)
    mixT = mix_region.bitcast(BF)
    mixT3 = v3(mixT, 8)
    AR2 = Arena(mix_region, 8 * NQ * 2)

    def alloc_p1(nbytes, dtype=F32):
        rounded = (nbytes + 31) // 32 * 32
        if AR2.off + rounded <= AR2.n:
            return AR2.alloc(nbytes, dtype)
        return AR.alloc(nbytes, dtype)

    def ld(dst, src, name):
        P.dma("sp", "const", 16, lambda e, d=dst, s=src: e.dma_start(out=d, in_=s), writes=[name])

    ld(cm, consts_in[:, :], "cm")
    ld(IDENTF, identf_in[:, :], "identf")
    ld(g_pre[:, 0:8], gpre[:, :], "g_pre")
    ld(g_post[:, 0:8], gpost[:, :], "g_post")
    ld(g_ffn[:, 0:8], gffn[:, :], "g_ffn")
    ld(g_post2[:, 0:8], gpost2[:, :], "g_post2")
    ld(cw[:, 0:132], cw_in[:, :], "cw")
    ld(cb[:, 0:44], cb_in[:, :], "cb")
    ld(lam_t[:, 0:256], lamv[:, :], "lam_t")
    ld(g_sub[:, 0:1], gsub[:, :], "g_sub")
    ld(g_sb[:, 0:1], gsb[:, :], "g_sb")
    ld(b31_t[:, 0:4], b31[:, :], "b31")
    ld(hv[:, 0:NSLOT], hv_in[:, :], "hv")
    ld(kcol[:, 0:NBLK], kcol_in[:, :], "kcol")

    P.op("dve", lambda e: e.tensor_tensor(out=lamtmp[:, 0:64], in0=lam_t[:, 0:64], in1=lam_t[:, 64:128], op=ALU.mult),
         reads=["lam_t"], writes=["lamtmp0"])
    P.op("dve", lambda e: e.tensor_tensor(out=lamtmp[:, 64:128], in0=lam_t[:, 128:192], in1=lam_t[:, 192:256], op=ALU.mult),
         reads=["lam_t"], writes=["lamtmp1"])
    P.op("dve", lambda e: e.reduce_sum(out=lamred[:, 0:1], in_=lamtmp[:, 0:64], axis=mybir.AxisListType.X),
         reads=["lamtmp0"], writes=["lamred0"])
    P.op("dve", lambda e: e.reduce_sum(out=lamred[:, 1:2], in_=lamtmp[:, 64:128], axis=mybir.AxisListType.X),
         reads=["lamtmp1"], writes=["lamred1"])
    P.op("act", lambda e: e.activation(out=lamred[:, 2:4], in_=lamred[:, 0:2], func=AF.Exp),
         reads=["lamred0", "lamred1"], writes=["lamexp"])
    P.op("dve", lambda e: e.scalar_tensor_tensor(out=neglam[:, 0:1], in0=lamred[:, 3:4], scalar=-LAM_INIT,
                                                 in1=lamred[:, 2:3], op0=ALU.add, op1=ALU.subtract),
         reads=["lamexp"], writes=["neglam"])
    P.op("dve", lambda e: e.tensor_scalar(out=gsub8[:, 0:1], in0=g_sub[:, 0:1], scalar1=(1.0 - LAM_INIT), scalar2=None,
                                          op0=ALU.mult), reads=["g_sub"], writes=["gsub8"])

    base_mark = AR.mark()

    Qd = v3(AR.alloc(4 * NQ * 2, BF), 4)
    Qs = v3(AR.alloc(4 * NQ * 2, BF), 4)
    p1_mark = AR.mark()
    wq = v3(AR.alloc(8 * 1024 * 2, BF), 8)
    wk = v3(AR.alloc(8 * 1024 * 2, BF), 8)
    wv = v3(AR.alloc(8 * 1024 * 2, BF), 8)
    wst = [v3(AR.alloc(8 * 256 * 4), 8) for _ in range(2)]

    wqkv_v = w_qkv.rearrange("(c p) n -> p c n", p=128)
    wplan = [(wq, 0, 0, 0.125), (wk, 0, 512, 1.0), (wv, 0, 1024, 1.0), (wq, 512, 1536, 0.125), (wk, 512, 2048, 1.0), (wv, 512, 2560, 1.0)]
    li = 0
    for (wdst, dcol, scol, scl) in wplan:
        for half in range(2):
            buf = wst[li % 2]
            c0 = scol + half * 256
            P.dma("sp", "wst", 2, lambda e, buf=buf, c0=c0: e.dma_start(out=buf[:, :, :], in_=wqkv_v[:, :, c0:c0 + 256]),
                  writes=[("wst", li % 2)])
            for fc in range(8):
                P.op("dve", lambda e, buf=buf, fc=fc, wdst=wdst, d0=dcol + half * 256, scl=scl: e.tensor_scalar(
                    out=wdst[:, fc, d0:d0 + 256], in0=buf[:, fc, :], scalar1=g_pre[:, fc:fc + 1], scalar2=scl, op0=ALU.mult, op1=ALU.mult),
                    reads=[("wst", li % 2), "g_pre"], writes=["wproj"])
            li += 1

    xt = [alloc_p1(1024 * 4) for _ in range(3)]
    xs = [alloc_p1(1024 * 2, BF) for _ in range(2)]
    junk = alloc_p1(1024 * 2, BF)
    ssq = AR.alloc(64 * 4)
    hT = [v3(AR.alloc(8 * 512 * 2, BF), 8) for _ in range(2)]
    kst = [v3(AR.alloc(8 * 512 * 2, BF), 8) for _ in range(2)]
    vstd = [alloc_p1(4 * 4 * 128 * 2, BF) for _ in range(2)]
    vsts = [AR.alloc(8 * 4 * 128 * 2, BF) for _ in range(2)]
    for i in range(2):
        P.op("pool", lambda e, i=i: e.memset(vsts[i][:, :], 0.0), writes=[("vsts", i)])

    cnt = {"x": 0, "h": 0, "ev": 0}

    def x_to_hT(src_rows_ap, n, hbuf_i, col0):
        i = cnt["x"]
        cnt["x"] += 1
        xb = xt[i % 3]
        xsb = xs[i % 2]
        sc = ssq[:, (i % 16) * 4:(i % 16) * 4 + 4]
        P.dma("sp", "xt", 3, lambda e: e.dma_start(out=xb[0:n, :], in_=src_rows_ap), writes=[("xt", i % 3)])
        P.op("act", lambda e: e.activation(out=junk[0:n, :], in_=xb[0:n, :], func=AF.Square, accum_out=sc[0:n, 0:1]),
             reads=[("xt", i % 3)], writes=["junk", ("ss", i % 16)])
        P.op("act", lambda e: e.activation(out=sc[0:n, 1:2], in_=sc[0:n, 0:1], func=AF.Sqrt, scale=1.0 / D, bias=eps_t[0:n, 0:1]),
             reads=[("ss", i % 16), "eps"], writes=[("sd", i % 16)])
        P.op("dve", lambda e: e.reciprocal(out=sc[0:n, 2:3], in_=sc[0:n, 1:2]), reads=[("sd", i % 16)], writes=[("rs", i % 16)])
        P.op("act", lambda e: e.activation(out=xsb[0:n, :], in_=xb[0:n, :], func=AF.Copy, scale=sc[0:n, 2:3]),
             reads=[("xt", i % 3), ("rs", i % 16)], writes=[("xs", i % 2)])
        pb = 6 + (i % 2)
        tp = v3(PSB[pb], 8)
        for f in range(8):
            P.op("pe", lambda e, f=f: e.transpose(out=tp[:, f, 0:n], in_=xsb[0:n, f * 128:(f + 1) * 128], identity=IDENT[0:n, 0:n]),
                 reads=[("xs", i % 2), "cm"], writes=[("ps", pb)])
        P.op("dve", lambda e: e.tensor_copy(out=hT[hbuf_i][:, :, col0:col0 + n], in_=tp[:, :, 0:n]),
             reads=[("ps", pb)], writes=[("hT", hbuf_i)])


    def evac(dst, src, reads, writes):
        i = cnt["ev"]
        cnt["ev"] += 1
        if i % 3 == 2:
            P.op("act", lambda e: e.activation(out=dst, in_=src, func=AF.Copy), reads=reads, writes=writes)
        else:
            P.op("dve", lambda e: e.tensor_copy(out=dst, in_=src), reads=reads, writes=writes)

    def q_slot(m, hi):
        r0 = 0
        while r0 < TC:
            n = min(128, TC - r0)
            x_to_hT(xq[m * TC + r0:m * TC + r0 + n, :], n, hi, r0)
            r0 += n
        for g in range(8):
            pb = g % 4
            for fc in range(8):
                P.op("pe", lambda e, fc=fc, g=g, pb=pb: e.matmul(
                    PSF[pb][:, 0:TC], lhsT=wq[:, fc, g * 128:(g + 1) * 128], rhs=hT[hi][:, fc, 0:TC],
                    start=(fc == 0), stop=(fc == 7)),
                    reads=[("hT", hi), "wproj"], writes=[("ps", pb)])
            dst = (Qd if g < 4 else Qs)[:, g % 4, m * TC:(m + 1) * TC]
            evac(dst, PSF[pb][:, 0:TC], [("ps", pb)], ["Q"])

    for m in range(NSLOT):
        q_slot(m, cnt["h"] % 2)
        cnt["h"] += 1

    kt_v = kt_scr.rearrange("(c p) t -> p c t", p=128)
    vd_v = vd_scr.rearrange("u p (b d) -> p u b d", d=128)
    vs_v = vs_scr.rearrange("u p (b d) -> p u b d", d=128)
    NSUP = SEQ // 512

    def kv_super(s, hi):
        for r in range(4):
            x_to_hT(x_rev[s * 512 + r * 128:s * 512 + (r + 1) * 128, :], 128, hi, r * 128)
        ks = kst[s % 2]
        for dc in range(8):
            pb = dc % 4
            for fc in range(8):
                P.op("pe", lambda e, fc=fc, dc=dc, pb=pb: e.matmul(
                    PSF[pb][:, 0:512], lhsT=wk[:, fc, dc * 128:(dc + 1) * 128], rhs=hT[hi][:, fc, 0:512],
                    start=(fc == 0), stop=(fc == 7)),
                    reads=[("hT", hi), "wproj"], writes=[("ps", pb)])
            evac(ks[:, dc, :], PSF[pb][:, 0:512], [("ps", pb)], [("kst", s % 2)])
        P.dma("pool", "kst", 2, lambda e: e.dma_start(out=kt_v[:, :, s * 512:(s + 1) * 512], in_=ks[:, :, :]),
              reads=[("kst", s % 2)], writes=["kt_scr"])
        vd = vstd[s % 2].rearrange("p (u b d) -> p u b d", u=4, b=4)
        vs = vsts[s % 2].rearrange("p (u b d) -> p u b d", u=8, b=4)
        for r in range(4):
            for cg in range(2):
                pb = 4 + cg
                for fc in range(8):
                    P.op("pe", lambda e, fc=fc, cg=cg, pb=pb, r=r: e.matmul(
                        PSF[pb][:, 0:512], lhsT=hT[hi][:, fc, r * 128:(r + 1) * 128], rhs=wv[:, fc, cg * 512:(cg + 1) * 512],
                        start=(fc == 0), stop=(fc == 7)),
                        reads=[("hT", hi), "wproj"], writes=[("ps", pb)])
                if cg == 0:
                    evac(vd[:, :, r, :], PSF[pb][:, 0:512].rearrange("p (u d) -> p u d", u=4), [("ps", pb)], [("vstd", s % 2)])
                else:
                    src = PSF[pb][:, 0:512].rearrange("p (u d) -> p u d", u=8)
                    for par in range(2):
                        evac(vs[:, par::2, r, par * 64:par * 64 + 64], src[:, par::2, :], [("ps", pb)], [("vsts", s % 2)])
        P.dma("pool", "vst", 2, lambda e: e.dma_start(out=vd_v[:, :, s * 4:(s + 1) * 4, :], in_=vd),
              reads=[("vstd", s % 2)], writes=["vd_scr"])
        P.dma("pool", "vst2", 2, lambda e: e.dma_start(out=vs_v[:, :, s * 4:(s + 1) * 4, :], in_=vs),
              reads=[("vsts", s % 2)], writes=["vs_scr"])

    for s_ in range(NSUP):
        kv_super(s_, cnt["h"] % 2)
        cnt["h"] += 1

    P.barrier()

    AR.reset(p1_mark)
    KT = AR.alloc(SEQ * 2, BF)
    Vb = v3(AR.alloc(NBLK * 128 * 2, BF), NBLK)
    posrow = AR.alloc(NQ * 4)
    Eb = [AR.alloc(TC * 4) for _ in range(2)]
    Lb = [AR.alloc(TC * 2, BF) for _ in range(2)]
    Ls = [AR.alloc(TC * 2, BF) for _ in range(2)]
    Ab = [AR.alloc(TC * 2, BF) for _ in range(2)]
    Mk = [AR.alloc(TC * 2, BF) for _ in range(3)]
    Bt = [AR.alloc(TC * 4) for _ in range(3)]
    Tt = [AR.alloc(TC * 4) for _ in range(2)]
    Pb = [AR.alloc(TC * 2, BF) for _ in range(3)]
    Osb = [AR.alloc(TC * 4) for _ in range(4)]
    fin = [AR.alloc(TC * 4) for _ in range(4)]
    finb = AR.alloc(TC * 2, BF)
    lnt = AR.alloc(TC * 4)
    P.dma("sp", "posrow", 1, lambda e: e.dma_start(out=posrow[:, 0:NQ], in_=posrow_in[:, :]), writes=["posrow"])

    def cols(a, n=TC):
        return a[:, 0:n]

    steps_sb = []
    for m in range(NSLOT):
        for kb in range(cfg.NB[m] - 1, -1, -1):
            steps_sb.append((m, kb, kb == cfg.NB[m] - 1, kb == 0, cfg.sb_masked(m, kb)))

    def sb_head(h):
        hh = h % 2
        rs = slice(hh * 64, hh * 64 + 64)
        P.dma("sp", "kt", 1, lambda e: e.dma_start(out=KT[rs, :], in_=kt_scr[512 + 64 * h:512 + 64 * h + 64, :]),
              reads=["kt_scr"], writes=["KT"])
        P.dma("sp", "vb", 1, lambda e: e.dma_start(out=Vb[:, :, :], in_=vs_scr[h].rearrange("p (b d) -> p b d", d=128)),
              reads=["vs_scr"], writes=["V"])
        n = len(steps_sb)

        def st_M(k):
            m, kb, first, last, masked = steps_sb[k]
            if masked:
                P.op("dve", lambda e: e.tensor_scalar(out=cols(Mk[k % 3]), in0=posrow[:, m * TC:(m + 1) * TC],
                                                      scalar1=kcol[:, kb:kb + 1], scalar2=0.0, op0=ALU.subtract, op1=ALU.min),
                     reads=["posrow", "kcol"], writes=[("Mk", k % 3)])

        def qk(k, pb, close):
            m, kb, first, last, masked = steps_sb[k]
            P.op("pe", lambda e: e.matmul(PSF[pb][:, 0:TC], lhsT=KT[rs, kb * 128:(kb + 1) * 128], rhs=Qs[rs, h // 2, m * TC:(m + 1) * TC],
                                          start=True, stop=(close and not masked)),
                 reads=["KT", "Q"], writes=[("ps", pb)])
            if masked:
                P.op("pe", lambda e: e.matmul(PSF[pb][:, 0:TC], lhsT=IDENT, rhs=cols(Mk[k % 3]), start=False, stop=close),
                     reads=[("Mk", k % 3), "cm"], writes=[("ps", pb)])

        def st_Z(k):
            qk(k, k % 2, True)

        def st_E(k):
            P.op("act", lambda e: e.activation(out=cols(Eb[k % 2]), in_=PSF[k % 2][:, 0:TC], func=AF.Exp),
                 reads=[("ps", k % 2)], writes=[("E", k % 2)])

        def st_L(k):
            P.op("act", lambda e: e.activation(out=cols(Lb[k % 2]), in_=cols(Eb[k % 2]), func=AF.Ln, bias=one_t[:, 0:1], scale=1.0),
                 reads=[("E", k % 2), "one"], writes=[("L", k % 2)])

        def st_LS(k):
            m, kb, first, last, masked = steps_sb[k]
            if last:
                return
            if first:
                P.op("pool", lambda e: e.tensor_copy(out=cols(Ls[(k + 1) % 2]), in_=cols(Lb[k % 2])),
                     reads=[("L", k % 2)], writes=[("Ls", (k + 1) % 2)])
            else:
                P.op("pool", lambda e: e.tensor_tensor(out=cols(Ls[(k + 1) % 2]), in0=cols(Ls[k % 2]), in1=cols(Lb[k % 2]), op=ALU.add),
                     reads=[("L", k % 2), ("Ls", k % 2)], writes=[("Ls", (k + 1) % 2)])

        def st_ZC(k):
            m, kb, first, last, masked = steps_sb[k]
            pb = 2 + k % 2
            qk(k, pb, False)
            P.op("pe", lambda e: e.matmul(PSF[pb][:, 0:TC], lhsT=NEGTRI, rhs=cols(Lb[k % 2]), start=False, stop=first),
                 reads=[("L", k % 2), "cm"], writes=[("ps", pb)])
            if not first:
                P.op("pe", lambda e: e.matmul(PSF[pb][:, 0:TC], lhsT=NEGONES, rhs=cols(Ls[k % 2]), start=False, stop=True),
                     reads=[("Ls", k % 2), "cm"], writes=[("ps", pb)])

        def st_A(k):
            pb = 2 + k % 2
            P.op("act", lambda e: e.activation(out=cols(Ab[k % 2]), in_=PSF[pb][:, 0:TC], func=AF.Exp),
                 reads=[("ps", pb)], writes=[("A", k % 2)])

        def st_AV(k):
            m, kb, first, last, masked = steps_sb[k]
            ob = 4 + (m % 2)
            P.op("pe", lambda e: e.matmul(PSF[ob][:, 0:TC], lhsT=Vb[:, kb, :], rhs=cols(Ab[k % 2]), start=first, stop=last),
                 reads=["V", ("A", k % 2)], writes=[("ps", ob)])
            if last:
                finalize_sb(m, ob)

        def finalize_sb(m, ob):
            P.op("dve", lambda e: e.tensor_copy(out=cols(fin[0]), in_=PSF[ob][:, 0:TC]), reads=[("ps", ob)], writes=[("fin", 0)])
            P.op("dve", lambda e: e.tensor_tensor(out=cols(finb), in0=cols(fin[0]), in1=cols(fin[0]), op=ALU.mult),
                 reads=[("fin", 0)], writes=["finb"])
            P.op("pe", lambda e: e.matmul(PSF[6][:, 0:TC], lhsT=ONESBLK, rhs=cols(finb), start=True, stop=True),
                 reads=["finb", "cm"], writes=[("ps", 6)])
            P.op("act", lambda e: e.activation(out=cols(lnt), in_=PSF[6][:, 0:TC], func=AF.Ln, scale=1.0 / HD, bias=eps_t[:, 0:1]),
                 reads=[("ps", 6), "eps"], writes=["lnt"])
            P.op("act", lambda e: e.activation(out=cols(fin[1]), in_=cols(lnt), func=AF.Exp, scale=-0.5),
                 reads=["lnt"], writes=[("fin", 1)])
            P.op("dve", lambda e: e.scalar_tensor_tensor(out=mixT3[rs, 4 + h // 2, m * TC:(m + 1) * TC], in0=fin[0][rs, 0:TC],
                                                         scalar=g_sb[rs, 0:1], in1=fin[1][rs, 0:TC], op0=ALU.mult, op1=ALU.mult),
                 reads=[("fin", 0), ("fin", 1), "g_sb"], writes=["mixT"])

        stages = [(st_L, 2), (st_LS, 2), (st_ZC, 2), (st_AV, 3), (st_M, 0), (st_Z, 0), (st_E, 1), (st_A, 2)]
        for t in range(n + 3):
            for fn, lag in stages:
                k = t - lag
                if 0 <= k < n:
                    fn(k)

    for h_ in range(8):
        sb_head(h_)

    steps_df = []
    for m in range(NSLOT):
        for kb in range(cfg.NB[m]):
            steps_df.append((m, kb, kb == 0, kb == cfg.NB[m] - 1, cfg.df_masked(m, kb)))

    def df_head(h):
        P.dma("sp", "kt", 1, lambda e: e.dma_start(out=KT[:, :], in_=kt_scr[128 * h:128 * h + 128, :]),
              reads=["kt_scr"], writes=["KT"])
        P.dma("sp", "vb", 1, lambda e: e.dma_start(out=Vb[:, :, :], in_=vd_scr[h].rearrange("p (b d) -> p b d", d=128)),
              reads=["vd_scr"], writes=["V"])
        n = len(steps_df)
        n2 = 2 * n

        def sd_B(kk):
            k, mm = kk // 2, kk % 2
            m, kb, first, last, masked = steps_df[k]
            if masked and mm == 0:
                src = bass.AP(tensor=gtab.tensor, offset=h * cfg.TABLEN + cfg.base(m, kb), ap=[[1, 128], [1, TC]])
                P.dma("sp", "bt", 3, lambda e: e.dma_start(out=cols(Bt[k % 3]), in_=src), writes=[("Bt", k % 3)])

        def sd_S(kk):
            k, mm = kk // 2, kk % 2
            m, kb, first, last, masked = steps_df[k]
            pb = kk % 2
            rm = slice(mm * 64, mm * 64 + 64)
            P.op("pe", lambda e: e.matmul(PSF[pb][:, 0:TC], lhsT=KT[rm, kb * 128:(kb + 1) * 128],
                                          rhs=Qd[rm, h, m * TC:(m + 1) * TC], start=True, stop=True),
                 reads=["KT", "Q"], writes=[("ps", pb)])

        def sd_T(kk):
            k, mm = kk // 2, kk % 2
            m, kb, first, last, masked = steps_df[k]
            if masked:
                P.op("dve", lambda e: e.tensor_tensor(out=cols(Tt[kk % 2]), in0=PSF[kk % 2][:, 0:TC], in1=cols(Bt[k % 3]), op=ALU.add),
                     reads=[("ps", kk % 2), ("Bt", k % 3)], writes=[("Tt", kk % 2)])

        def sd_P(kk):
            k, mm = kk // 2, kk % 2
            m, kb, first, last, masked = steps_df[k]
            if masked:
                P.op("act", lambda e: e.activation(out=cols(Pb[kk % 3]), in_=cols(Tt[kk % 2]), func=AF.Exp),
                     reads=[("Tt", kk % 2)], writes=[("Pb", kk % 3)])
            else:
                P.op("act", lambda e: e.activation(out=cols(Pb[kk % 3]), in_=PSF[kk % 2][:, 0:TC], func=AF.Exp, bias=b31_t[:, h:h + 1], scale=1.0),
                     reads=[("ps", kk % 2), "b31"], writes=[("Pb", kk % 3)])

        def sd_PV(kk):
            k, mm = kk // 2, kk % 2
            m, kb, first, last, masked = steps_df[k]
            ob = 2 + mm
            lb = 4 + mm
            P.op("pe", lambda e: e.matmul(PSF[ob][:, 0:TC], lhsT=Vb[:, kb, :], rhs=cols(Pb[kk % 3]), start=first, stop=last),
                 reads=["V", ("Pb", kk % 3)], writes=[("ps", ob)])
            P.op("pe", lambda e: e.matmul(PSF[lb][:, 0:TC], lhsT=ONES, rhs=cols(Pb[kk % 3]), start=first, stop=last),
                 reads=[("Pb", kk % 3), "cm"], writes=[("ps", lb)])
            if last and mm == 1:
                finalize_df(m)

        def finalize_df(m):
            for q in range(2):
                P.op("dve", lambda e, q=q: e.tensor_copy(out=cols(Osb[q]), in_=PSF[2 + q][:, 0:TC]), reads=[("ps", 2 + q)], writes=[("Osb", q)])
                P.op("dve", lambda e, q=q: e.tensor_scalar(out=cols(Osb[2 + q]), in0=PSF[4 + q][:, 0:TC], scalar1=1e-30, scalar2=None, op0=ALU.max),
                     reads=[("ps", 4 + q)], writes=[("Osb", 2 + q)])
            for q in range(2):
                P.op("dve", lambda e, q=q: e.reciprocal(out=cols(fin[2 + q]), in_=cols(Osb[2 + q])), reads=[("Osb", 2 + q)], writes=[("fin", 2 + q)])
                P.op("dve", lambda e, q=q: e.tensor_tensor(out=cols(fin[2 + q]), in0=cols(Osb[q]), in1=cols(fin[2 + q]), op=ALU.mult),
                     reads=[("Osb", q), ("fin", 2 + q)], writes=[("fin", 2 + q)])
            P.op("dve", lambda e: e.scalar_tensor_tensor(out=cols(fin[0]), in0=cols(fin[3]), scalar=neglam[:, 0:1], in1=cols(fin[2]),
                                                         op0=ALU.mult, op1=ALU.add),
                 reads=[("fin", 2), ("fin", 3), "neglam"], writes=[("fin", 0)])
            P.op("dve", lambda e: e.tensor_tensor(out=cols(finb), in0=cols(fin[0]), in1=cols(fin[0]), op=ALU.mult),
                 reads=[("fin", 0)], writes=["finb"])
            P.op("pe", lambda e: e.matmul(PSF[6][:, 0:TC], lhsT=ONES, rhs=cols(finb), start=True, stop=True),
                 reads=["finb", "cm"], writes=[("ps", 6)])
            P.op("act", lambda e: e.activation(out=cols(lnt), in_=PSF[6][:, 0:TC], func=AF.Ln, scale=1.0 / 128.0, bias=eps_t[:, 0:1]),
                 reads=[("ps", 6), "eps"], writes=["lnt"])
            P.op("act", lambda e: e.activation(out=cols(fin[1]), in_=cols(lnt), func=AF.Exp, scale=-0.5),
                 reads=["lnt"], writes=[("fin", 1)])
            P.op("dve", lambda e: e.scalar_tensor_tensor(out=mixT3[:, h, m * TC:(m + 1) * TC], in0=cols(fin[0]),
                                                         scalar=gsub8[:, 0:1], in1=cols(fin[1]), op0=ALU.mult, op1=ALU.mult),
                 reads=[("fin", 0), ("fin", 1), "gsub8"], writes=["mixT"])

        stages = [(sd_PV, 3), (sd_B, 0), (sd_S, 0), (sd_T, 1), (sd_P, 1)]
        for t in range(n2 + 3):
            for fn, lag in stages:
                kk = t - lag
                if 0 <= kk < n2:
                    fn(kk)

    for h_ in range(4):
        df_head(h_)

    P.barrier()

    AR.reset(base_mark)
    wo = v3(AR.alloc(8 * 1024 * 2, BF), 8)
    wd = v3(AR.alloc(NJ * 1024 * 2, BF), NJ)
    wu = [v3(AR.alloc(8 * 256 * 2, BF), 8) for _ in range(2)]
    gvT = v3(AR.alloc(NJ * TC * 2, BF), NJ)
    xT = v3(AR.alloc(8 * TC * 4), 8)
    Mo = v3(AR.alloc(8 * TC * 4), 8)
    sqb = v3(AR.alloc(8 * TC * 2, BF), 8)
    x1T = v3(AR.alloc(8 * TC * 4), 8)
    h2T = v3(AR.alloc(8 * TC * 2, BF), 8)
    rstd = AR.alloc(TC * 4)
    tmp3 = AR.alloc(TC * 4)
    yg = [AR.alloc(TC * 4) for _ in range(2)]
    yv = [AR.alloc(TC * 4) for _ in range(2)]
    gl = [AR.alloc(TC * 4) for _ in range(2)]
    xrow = [AR.alloc(1024 * 4) for _ in range(2)]

    P.dma("pool", "wo", 1, lambda e: e.dma_start(out=wo[:, :, :], in_=w_o.rearrange("(c p) n -> p c n", p=128)), writes=["wo"])
    P.dma("pool", "wd", 1, lambda e: e.dma_start(out=wd[:, :, :], in_=w_down.rearrange("(c p) n -> p c n", p=128)), writes=["wd"])
    wup_v = w_up.rearrange("(c p) n -> p c n", p=128)
    RB = 1

    def rms_rstd(src3, nfeat, tagname):
        for dc in range(8):
            P.op("pool", lambda e, dc=dc: e.tensor_tensor(out=sqb[:, dc, :], in0=src3[:, dc, :], in1=src3[:, dc, :], op=ALU.mult),
                 reads=[tagname], writes=["sqb"])
        for dc in range(8):
            P.op("pe", lambda e, dc=dc: e.matmul(PSF[RB][:, 0:TC], lhsT=ONES, rhs=sqb[:, dc, :], start=(dc == 0), stop=(dc == 7)),
                 reads=["sqb", "cm"], writes=[("ps", RB)])
        P.op("act", lambda e: e.activation(out=cols(tmp3), in_=PSF[RB][:, 0:TC], func=AF.Ln, scale=1.0 / nfeat, bias=eps_t[:, 0:1]),
             reads=[("ps", RB), "eps"], writes=["tmp3"])
        P.op("act", lambda e: e.activation(out=cols(rstd), in_=cols(tmp3), func=AF.Exp, scale=-0.5), reads=["tmp3"], writes=["rstd"])

    jcount = {"n": 0, "row": 0}

    def ffn_j(m, j):
        ji = jcount["n"]
        jcount["n"] += 1
        wb = wu[ji % 2]
        P.dma("pool", "wu", 2, lambda e: e.dma_start(out=wb[:, :, 0:128], in_=wup_v[:, :, j * 128:(j + 1) * 128]),
              writes=[("wu", ji % 2)])
        P.dma("pool", "wu2", 2, lambda e: e.dma_start(out=wb[:, :, 128:256], in_=wup_v[:, :, DFF + j * 128:DFF + (j + 1) * 128]),
              writes=[("wu", ji % 2)])
        for half in range(2):
            pb = 4 + 2 * (ji % 2) + half
            for fc in range(8):
                P.op("pe", lambda e, fc=fc, half=half, pb=pb: e.matmul(
                    PSF[pb][:, 0:TC], lhsT=wb[:, fc, half * 128:(half + 1) * 128], rhs=h2T[:, fc, :], start=(fc == 0), stop=(fc == 7)),
                    reads=[("wu", ji % 2), "h2T"], writes=[("ps", pb)])
            ydst = (yg if half == 0 else yv)[ji % 2]
            ch = j + half * NJ
            yname = ("y", half, ji % 2)
            P.op("dve", lambda e, ydst=ydst, pb=pb, ch=ch: e.tensor_scalar(
                out=cols(ydst), in0=PSF[pb][:, 0:TC], scalar1=cw[:, ch * 3 + 2:ch * 3 + 3], scalar2=cb[:, ch:ch + 1], op0=ALU.mult, op1=ALU.add),
                reads=[("ps", pb), "cw", "cb"], writes=[yname])
            P.op("dve", lambda e, ydst=ydst, pb=pb, ch=ch: e.scalar_tensor_tensor(
                out=ydst[:, 1:TC], in0=PSF[pb][:, 0:TC - 1], scalar=cw[:, ch * 3 + 1:ch * 3 + 2], in1=ydst[:, 1:TC], op0=ALU.mult, op1=ALU.add),
                reads=[("ps", pb), "cw", yname], writes=[yname])
            P.op("dve", lambda e, ydst=ydst, pb=pb, ch=ch: e.scalar_tensor_tensor(
                out=ydst[:, 2:TC], in0=PSF[pb][:, 0:TC - 2], scalar=cw[:, ch * 3 + 0:ch * 3 + 1], in1=ydst[:, 2:TC], op0=ALU.mult, op1=ALU.add),
                reads=[("ps", pb), "cw", yname], writes=[yname])
        P.op("act", lambda e: e.activation(out=cols(gl[ji % 2]), in_=cols(yg[ji % 2]), func=AF.Gelu_apprx_tanh),
             reads=[("y", 0, ji % 2)], writes=[("gl", ji % 2)])
        P.op("pool", lambda e: e.tensor_tensor(out=gvT[:, j, :], in0=cols(gl[ji % 2]), in1=cols(yv[ji % 2]), op=ALU.mult),
             reads=[("gl", ji % 2), ("y", 1, ji % 2)], writes=["gvT"])

    def p3_slot(m):
        r0 = 0
        while r0 < TC:
            n = min(128, TC - r0)
            ri = jcount["row"]
            jcount["row"] += 1
            xb = xrow[ri % 2]
            P.dma("sp", "xrow", 2, lambda e, xb=xb, n=n, r0=r0: e.dma_start(out=xb[0:n, :], in_=xq[m * TC + r0:m * TC + r0 + n, :]),
                  writes=[("xrow", ri % 2)])
            for half in range(2):
                pb = 2 + half
                for f4 in range(4):
                    f = half * 4 + f4
                    P.op("pe", lambda e, xb=xb, n=n, f=f, f4=f4, pb=pb: e.transpose(
                        out=PSF[pb][:, f4 * 128:f4 * 128 + n], in_=xb[0:n, f * 128:(f + 1) * 128], identity=IDENTF[0:n, 0:n]),
                        reads=[("xrow", ri % 2), "identf"], writes=[("ps", pb)])
                P.op("dve", lambda e, n=n, r0=r0, half=half, pb=pb: e.tensor_copy(
                    out=xT[:, half * 4:half * 4 + 4, r0:r0 + n], in_=PSF[pb][:, :].rearrange("p (a b) -> p a b", a=4)[:, :, 0:n]),
                    reads=[("ps", pb)], writes=["xT"])
            r0 += n
        for dc in range(8):
            pb = 2 + dc % 2
            for fc in range(8):
                P.op("pe", lambda e, dc=dc, fc=fc, pb=pb: e.matmul(PSF[pb][:, 0:TC], lhsT=wo[:, fc, dc * 128:(dc + 1) * 128],
                                                                   rhs=mixT3[:, fc, m * TC:(m + 1) * TC], start=(fc == 0), stop=(fc == 7)),
                     reads=["wo", "mixT"], writes=[("ps", pb)])
            P.op("dve", lambda e, dc=dc, pb=pb: e.tensor_copy(out=Mo[:, dc, :], in_=PSF[pb][:, 0:TC]), reads=[("ps", pb)], writes=["Mo"])
        rms_rstd(Mo, D, "Mo")
        for dc in range(8):
            P.op("dve", lambda e, dc=dc: e.scalar_tensor_tensor(out=Mo[:, dc, :], in0=Mo[:, dc, :], scalar=g_post[:, dc:dc + 1], in1=cols(rstd),
                                                                op0=ALU.mult, op1=ALU.mult), reads=["Mo", "rstd", "g_post"], writes=["Mo"])
            P.op("dve", lambda e, dc=dc: e.tensor_tensor(out=x1T[:, dc, :], in0=Mo[:, dc, :], in1=xT[:, dc, :], op=ALU.add),
                 reads=["Mo", "xT"], writes=["x1T"])
        rms_rstd(x1T, D, "x1T")
        for dc in range(8):
            P.op("dve", lambda e, dc=dc: e.scalar_tensor_tensor(out=h2T[:, dc, :], in0=x1T[:, dc, :], scalar=g_ffn[:, dc:dc + 1], in1=cols(rstd),
                                                                op0=ALU.mult, op1=ALU.mult), reads=["x1T", "rstd", "g_ffn"], writes=["h2T"])
        P.op("dve", lambda e: e.tensor_scalar(out=h2T[:, :, 0:2], in0=h2T[:, :, 0:2], scalar1=hv[:, m:m + 1], scalar2=None, op0=ALU.mult),
             reads=["h2T", "hv"], writes=["h2T"])
        for j in range(NJ):
            ffn_j(m, j)
        for dc in range(8):
            pb = 2 + dc % 2
            for j in range(NJ):
                P.op("pe", lambda e, dc=dc, j=j, pb=pb: e.matmul(PSF[pb][:, 0:TC], lhsT=wd[:, j, dc * 128:(dc + 1) * 128], rhs=gvT[:, j, :],
                                                                 start=(j == 0), stop=(j == NJ - 1)),
                     reads=["wd", "gvT"], writes=[("ps", pb)])
            P.op("dve", lambda e, dc=dc, pb=pb: e.tensor_copy(out=Mo[:, dc, :], in_=PSF[pb][:, 0:TC]), reads=[("ps", pb)], writes=["Mo"])
        rms_rstd(Mo, D, "Mo")
        for dc in range(8):
            P.op("dve", lambda e, dc=dc: e.scalar_tensor_tensor(out=Mo[:, dc, :], in0=Mo[:, dc, :], scalar=g_post2[:, dc:dc + 1], in1=cols(rstd),
                                                                op0=ALU.mult, op1=ALU.mult), reads=["Mo", "rstd", "g_post2"], writes=["Mo"])
            P.op("dve", lambda e, dc=dc: e.tensor_tensor(out=xT[:, dc, :], in0=Mo[:, dc, :], in1=x1T[:, dc, :], op=ALU.add),
                 reads=["Mo", "x1T"], writes=["xT"])
        r0 = 0
        while r0 < TC:
            n = min(128, TC - r0)
            ri = jcount["row"]
            jcount["row"] += 1
            ob_ = xrow[ri % 2]
            for half in range(2):
                pb = 2 + half
                for f4 in range(4):
                    f = half * 4 + f4
                    P.op("pe", lambda e, n=n, r0=r0, f=f, f4=f4, pb=pb: e.transpose(
                        out=PSF[pb][0:n, f4 * 128:(f4 + 1) * 128], in_=xT[:, f, r0:r0 + n], identity=IDENTF[:, :]),
                        reads=["xT", "identf"], writes=[("ps", pb)])
                P.op("dve", lambda e, n=n, half=half, pb=pb, ob_=ob_: e.tensor_copy(out=ob_[0:n, half * 512:(half + 1) * 512], in_=PSF[pb][0:n, 0:512]),
                     reads=[("ps", pb)], writes=[("xrow", ri % 2)])
            o = P.dma("sp", "xrow", 2, lambda e, n=n, r0=r0, ob_=ob_: e.dma_start(out=out[m * TC + r0:m * TC + r0 + n, :], in_=ob_[0:n, :]),
                      reads=[("xrow", ri % 2)], writes=["out"])
            P.out_dmas.append(o)
            r0 += n

    for m_ in range(NSLOT):
        p3_slot(m_)

    if debug:
        o = P.dma("sp", "dbg", 1, lambda e: e.dma_start(out=dbg[:, :], in_=mixT[:, :]), reads=["mixT"], writes=["dbg"])
        P.out_dmas.append(o)

    P.emit()
    st.close()
    return nc


def _bucket(n):
    n = np.maximum(n, 0)
    nf = np.maximum(n, 1).astype(np.float32)
    large = 16 + (np.log(nf / np.float32(16)) / np.float32(math.log(128 / 16)) * np.float32(16)).astype(np.int32)
    large = np.minimum(large, 31)
    return np.where(n < 16, n, large)


def make_inputs(cfg, inp):
    SEQ, NSLOT, W, TC, NBLK, NQ = cfg.SEQ, cfg.NSLOT, cfg.W, cfg.TC, cfg.NBLK, cfg.NQ
    f32 = np.float32
    x = np.ascontiguousarray(np.asarray(inp["x"], f32)[0])
    x_rev = np.ascontiguousarray(x.reshape(NBLK, 128, D)[:, ::-1, :].reshape(SEQ, D))

    def pc(v, c):
        return np.ascontiguousarray(np.asarray(v, f32).reshape(c, 128).T)

    shared = {
        "x_rev": x_rev,
        "w_qkv": np.ascontiguousarray(np.asarray(inp["w_qkv"], f32)[0]),
        "w_o": np.ascontiguousarray(np.asarray(inp["w_o"], f32)[0]),
        "w_up": np.ascontiguousarray(np.asarray(inp["w_up"], f32)[0]),
        "w_down": np.ascontiguousarray(np.asarray(inp["w_down"], f32)[0]),
        "gpre": pc(inp["attn_pre_norm"][0], 8),
        "gpost": pc(inp["attn_post_norm"][0], 8),
        "gffn": pc(inp["ffn_pre_norm"][0], 8),
        "gpost2": pc(inp["ffn_post_norm"][0], 8),
        "cw": np.ascontiguousarray(np.asarray(inp["conv_w"], f32)[0].reshape(3, 44, 128).transpose(2, 1, 0).reshape(128, 132)),
        "cb": pc(inp["conv_b"][0], 44),
        "lamv": np.ascontiguousarray(np.broadcast_to(np.concatenate(
            [np.asarray(inp[k], f32)[0] for k in ("lambda_q1", "lambda_k1", "lambda_q2", "lambda_k2")])[None, :], (128, 256))),
        "gsub": np.ascontiguousarray(np.asarray(inp["diff_subln"], f32)[0].reshape(128, 1)),
        "gsb": np.ascontiguousarray(np.tile(np.asarray(inp["sb_norm"], f32)[0], 2).reshape(128, 1)),
        "b31": np.ascontiguousarray(np.broadcast_to(np.asarray(inp["rel_bias"], f32)[31][None, :], (128, 4))),
        "identf": np.eye(128, dtype=f32),
    }
    pj = np.arange(128)
    ident = np.eye(128, dtype=f32)
    negtri = -(pj[:, None] <= pj[None, :]).astype(f32)
    negones = -np.ones((128, 128), f32)
    ones = np.ones((128, 128), f32)
    onesblk = ((pj[:, None] // 64) == (pj[None, :] // 64)).astype(f32)
    shared["cmats"] = np.concatenate([ident, negtri, negones, ones, onesblk], axis=1).astype(ml_dtypes.bfloat16)
    kb = np.arange(NBLK)
    shared["kcol"] = np.ascontiguousarray(((128 * kb[None, :] + 127 - pj[:, None]) * BIG).astype(f32))

    rel_bias = np.asarray(inp["rel_bias"], f32)
    maps = []
    for c in range(NCORES):
        mp = dict(shared)
        idx = np.zeros(NQ, np.int64)
        pos = np.zeros(NQ, np.int64)
        hvv = np.ones((128, NSLOT), f32)
        for m in range(NSLOT):
            t = cfg.t0(c, m) + np.arange(TC)
            pos[m * TC:(m + 1) * TC] = t
            idx[m * TC:(m + 1) * TC] = np.clip(t, 0, SEQ - 1)
            if t[0] < 0:
                hvv[:, m] = 0.0
        mp["xq"] = np.ascontiguousarray(x[idx])
        mp["hv"] = hvv
        mp["posrow"] = np.ascontiguousarray(np.broadcast_to(((pos - 1) * BIG).astype(f32)[None, :], (128, NQ)))
        n = np.arange(cfg.TABLEN)
        d = n - cfg.OFF0 + W * c - 2 - 127
        tab = np.where(d[None, :] >= 0, rel_bias[_bucket(d), :].T, f32(NEG)).astype(f32)
        mp["gtab"] = np.ascontiguousarray(tab)
        maps.append(mp)
    return maps


def assemble(cfg, results):
    SEQ, NSLOT, W, TC = cfg.SEQ, cfg.NSLOT, cfg.W, cfg.TC
    outp = np.zeros((SEQ, D), np.float32)
    for c in range(NCORES):
        o = results[c]["out"]
        for m in range(NSLOT):
            t0 = W * (8 * m + c)
            n = min(W, SEQ - t0)
            if n <= 0:
                continue
            outp[t0:t0 + n] = o[m * TC + 2:m * TC + 2 + n]
    return outp[None]


_NC_CACHE = {}


def run(cfg, inp, debug=None, trace=False):
    key = (cfg.SEQ, cfg.NSLOT, cfg.W, debug)
    if key not in _NC_CACHE:
        _NC_CACHE[key] = build(cfg, debug)
    nc = _NC_CACHE[key]
    maps = make_inputs(cfg, inp)
    res = run_bass_kernel_spmd(nc, maps, core_ids=list(range(NCORES)), **({"trace": True} if trace else {}))
    return res


def kernel(**inputs):
    res = run(FULL, inputs)
    return assemble(FULL, res.results)
```

```python
import math
import numpy as np
import ml_dtypes
import concourse.bass as bass
import concourse.mybir as mybir
from concourse.bass_utils import run_bass_kernel_spmd

F32 = mybir.dt.float32
BF = mybir.dt.bfloat16
AF = mybir.ActivationFunctionType
ALU = mybir.AluOpType

NCORES = 8
WARM_SB = 0
WARM_DF = 0
D = 1024
HD = 64
EPS = 1e-6
LAM_INIT = 0.8 - 0.6 * math.exp(-0.3 * 0)
BIG = 32768.0
NEG = -30000.0
DFF = 2816
NJ = DFF // 128


class Cfg:
    def __init__(self, seq, nslot, w):
        self.SEQ = seq
        self.NSLOT = nslot
        self.W = w
        self.TC = w + 2
        self.NBLK = seq // 128
        self.NQ = nslot * self.TC
        assert 8 * nslot * w >= seq
        self.NB = [min(self.NBLK, -(-(8 * (m + 1) * w) // 128)) for m in range(nslot)]
        self.OFF0 = max(128 * (self.NB[m] - 1) - 8 * w * m for m in range(nslot)) + 160
        self.TABLEN = max(8 * w * m + self.OFF0 for m in range(nslot)) + self.TC + 256

    def t0(self, c, m):
        return self.W * (8 * m + c) - 2

    def sb_masked(self, m, kb):
        return (self.t0(0, m) - 1) - (128 * kb + 127) < 0

    def df_masked(self, m, kb):
        return self.t0(0, m) - (128 * kb + 127) < 113

    def base(self, m, kb):
        return 8 * self.W * m - 128 * kb + self.OFF0


FULL = Cfg(16384, 5, 410)


class Op:
    __slots__ = ("eng", "fn", "deps", "sig", "need", "dma", "idx", "tag")

    def __init__(self, eng, fn, dma=None, tag=""):
        self.eng = eng
        self.fn = fn
        self.deps = []
        self.sig = None
        self.need = False
        self.dma = dma
        self.tag = tag


class Prog:
    ENGS = ("pe", "act", "dve", "pool", "sp")

    def __init__(self, nc):
        self.nc = nc
        self.ops = {e: [] for e in self.ENGS}
        self.lastw = {}
        self.readers = {}
        self.dma_count = {}
        self.dma_depth = {}
        self.all_dma_since_barrier = []
        self.out_dmas = []

    def _add(self, op, reads, writes):
        deps = []
        for r in reads:
            w = self.lastw.get(r)
            if w is not None:
                deps.append(w)
        for w_ in writes:
            w = self.lastw.get(w_)
            if w is not None:
                deps.append(w)
            deps.extend(self.readers.get(w_, ()))
        seen = set()
        for d in deps:
            if d is op or id(d) in seen:
                continue
            seen.add(id(d))
            if d.eng == "pe" and op.eng == "pe" and d.dma is None and op.dma is None:
                continue
            op.deps.append(d)
            d.need = True
        for r in reads:
            self.readers.setdefault(r, []).append(op)
        for w_ in writes:
            self.lastw[w_] = op
            self.readers[w_] = []
        self.ops[op.eng].append(op)
        return op

    def op(self, eng, fn, reads=(), writes=(), tag=""):
        return self._add(Op(eng, fn, tag=tag), reads, writes)

    def dma(self, eng, stream, depth, fn, reads=(), writes=(), tag=""):
        n = self.dma_count.get(stream, 0)
        self.dma_count[stream] = n + 1
        self.dma_depth[stream] = depth
        o = Op(eng, fn, dma=(stream, n % depth, 16 * (n // depth + 1)), tag=tag)
        o.need = True
        self.all_dma_since_barrier.append(o)
        return self._add(o, reads, writes)

    def barrier(self):
        lasts = [self.ops[e][-1] for e in self.ENGS if self.ops[e]]
        dmas = list(self.all_dma_since_barrier)
        self.all_dma_since_barrier = []
        for e in self.ENGS:
            o = Op(e, None, tag="barrier")
            for d in lasts + dmas:
                if d.eng == e and d.dma is None:
                    continue
                o.deps.append(d)
                d.need = True
            self.ops[e].append(o)
        self.lastw = {}
        self.readers = {}

    def emit(self):
        nc = self.nc
        import contextlib
        with contextlib.ExitStack() as st:
            esem = {e: st.enter_context(nc.semaphore("sem_" + e)) for e in ("pe", "act", "dve", "pool")}
            dsem = {}
            for s, dep in self.dma_depth.items():
                for i in range(dep):
                    dsem[(s, i)] = st.enter_context(nc.semaphore("d_%s_%d" % (s, i)))
            cnt = {e: 0 for e in esem}
            for e in self.ENGS:
                for o in self.ops[e]:
                    if o.dma is not None:
                        o.sig = (dsem[(o.dma[0], o.dma[1])], o.dma[2])
                    elif o.need and o.fn is not None:
                        cnt[e] += 1
                        o.sig = (esem[e], cnt[e])
                    elif o.need and o.fn is None:
                        o.sig = None
            block = st.enter_context(nc.Block())
            prog = self

            def run(eng_name, eng):
                waited = {}
                for o in prog.ops[eng_name]:
                    need = {}
                    for d in o.deps:
                        if d.sig is None:
                            continue
                        sem, val = d.sig
                        k = id(sem)
                        if k not in need or need[k][1] < val:
                            need[k] = (sem, val)
                    for k, (sem, val) in need.items():
                        if waited.get(k, 0) >= val:
                            continue
                        waited[k] = val
                        eng.wait_ge(sem, val)
                    if o.fn is None:
                        continue
                    ins = o.fn(eng)
                    if o.sig is not None:
                        if o.dma is not None:
                            ins.then_inc(o.sig[0], 16)
                        else:
                            ins.then_inc(o.sig[0], 1)
                if eng_name == "sp":
                    for o in prog.out_dmas:
                        sem, val = o.sig
                        if waited.get(id(sem), 0) < val:
                            waited[id(sem)] = val
                            eng.wait_ge(sem, val)

            @block.tensor
            def _(e):
                run("pe", e)

            @block.scalar
            def _(e):
                run("act", e)

            @block.vector
            def _(e):
                run("dve", e)

            @block.gpsimd
            def _(e):
                run("pool", e)

            @block.sync
            def _(e):
                run("sp", e)


class Arena:
    def __init__(self, ap_f32, nbytes):
        self.ap = ap_f32
        self.n = nbytes
        self.off = 0
        self.marks = []

    def alloc(self, nbytes, dtype=F32):
        assert nbytes % 4 == 0
        rounded = (nbytes + 31) // 32 * 32
        assert self.off + rounded <= self.n, ("arena overflow", self.off, rounded, self.n)
        a = self.ap[:, self.off // 4:(self.off + nbytes) // 4]
        self.off += rounded
        if dtype == BF:
            a = a.bitcast(BF)
        return a

    def mark(self):
        return self.off

    def reset(self, m):
        self.off = m


def v3(ap, a):
    return ap.rearrange("p (a b) -> p a b", a=a)


def build(cfg, debug=None):
    SEQ, NSLOT, W, TC, NBLK, NQ = cfg.SEQ, cfg.NSLOT, cfg.W, cfg.TC, cfg.NBLK, cfg.NQ
    nc = bass.Bass("TRN2", target_bir_lowering=False)

    def din(name, shape, dt=F32):
        return nc.dram_tensor(name, list(shape), dt, kind="ExternalInput").ap()

    x_rev = din("x_rev", [SEQ, D])
    xq = din("xq", [NQ, D])
    w_qkv = din("w_qkv", [D, 3072])
    w_o = din("w_o", [D, D])
    w_up = din("w_up", [D, 2 * DFF])
    w_down = din("w_down", [DFF, D])
    gpre = din("gpre", [128, 8])
    gpost = din("gpost", [128, 8])
    gffn = din("gffn", [128, 8])
    gpost2 = din("gpost2", [128, 8])
    cw_in = din("cw", [128, 44 * 3])
    cb_in = din("cb", [128, 44])
    lamv = din("lamv", [128, 4 * 64])
    gsub = din("gsub", [128, 1])
    gsb = din("gsb", [128, 1])
    b31 = din("b31", [128, 4])
    hv_in = din("hv", [128, NSLOT])
    posrow_in = din("posrow", [128, NQ])
    kcol_in = din("kcol", [128, NBLK])
    gtab = din("gtab", [4, cfg.TABLEN])
    consts_in = din("cmats", [128, 5 * 128], BF)
    identf_in = din("identf", [128, 128])
    out = nc.dram_tensor("out", [NQ, D], F32, kind="ExternalOutput").ap()
    if debug:
        dbg = nc.dram_tensor("dbg", [128, 8 * NQ], BF, kind="ExternalOutput").ap()

    kt_scr = nc.dram_tensor("kt_scr", [D, SEQ], BF).ap()
    vd_scr = nc.dram_tensor("vd_scr", [4, 128, NBLK * 128], BF).ap()
    vs_scr = nc.dram_tensor("vs_scr", [8, 128, NBLK * 128], BF).ap()
    wup_scr = nc.dram_tensor("wup_scr", [NJ, 128, 8 * 256], BF).ap()
    wd_scr = nc.dram_tensor("wd_scr", [128, NJ * 1024], BF).ap()
    wo_scr = nc.dram_tensor("wo_scr", [128, 8 * 1024], BF).ap()

    import contextlib
    st = contextlib.ExitStack()
    ARENA_BYTES = 204 * 1024
    arena_t = st.enter_context(nc.sbuf_tensor("arena", [128, ARENA_BYTES // 4], F32))
    AR = Arena(arena_t[:, :], ARENA_BYTES)
    banks = [st.enter_context(nc.psum_tensor("bank%d" % i, [128, 512], F32)) for i in range(8)]
    PSF = [b[:, :] for b in banks]
    PSB = [b[:, :].bitcast(BF) for b in banks]

    P = Prog(nc)

    cm = AR.alloc(5 * 128 * 2, BF)
    IDENT = cm[:, 0:128]
    NEGTRI = cm[:, 128:256]
    NEGONES = cm[:, 256:384]
    ONES = cm[:, 384:512]
    ONESBLK = cm[:, 512:640]
    IDENTF = AR.alloc(128 * 4)
    g_pre = AR.alloc(8 * 4)
    g_post = AR.alloc(8 * 4)
    g_ffn = AR.alloc(8 * 4)
    g_post2 = AR.alloc(8 * 4)
    cw = AR.alloc(44 * 3 * 4)
    cb = AR.alloc(44 * 4)
    lam_t = AR.alloc(4 * 64 * 4)
    g_sub = AR.alloc(32)
    g_sb = AR.alloc(32)
    b31_t = AR.alloc(32)
    hv = AR.alloc(max(32, NSLOT * 4))
    neglam = AR.alloc(32)
    lamtmp = AR.alloc(4 * 64 * 4)
    lamred = AR.alloc(32)
    gsub8 = AR.alloc(32)
    kcol = AR.alloc(NBLK * 4)
    eps_t = AR.alloc(32)
    one_t = AR.alloc(32)
    P.op("pool", lambda e: e.memset(eps_t[:, 0:1], EPS), writes=["eps"])
    P.op("pool", lambda e: e.memset(one_t[:, 0:1], 1.0), writes=["one"])
    mix_region = AR.alloc(8 * NQ * 2)
    mixT = mix_region.bitcast(BF)
    mixT3 = v3(mixT, 8)
    AR2 = Arena(mix_region, 8 * NQ * 2)

    def alloc_p1(nbytes, dtype=F32):
        rounded = (nbytes + 31) // 32 * 32
        if AR2.off + rounded <= AR2.n:
            return AR2.alloc(nbytes, dtype)
        return AR.alloc(nbytes, dtype)

    def ld(dst, src, name):
        P.dma("sp", "const", 16, lambda e, d=dst, s=src: e.dma_start(out=d, in_=s), writes=[name])

    ld(cm, consts_in[:, :], "cm")
    ld(IDENTF, identf_in[:, :], "identf")
    ld(g_pre[:, 0:8], gpre[:, :], "g_pre")
    ld(g_post[:, 0:8], gpost[:, :], "g_post")
    ld(g_ffn[:, 0:8], gffn[:, :], "g_ffn")
    ld(g_post2[:, 0:8], gpost2[:, :], "g_post2")
    ld(cw[:, 0:132], cw_in[:, :], "cw")
    ld(cb[:, 0:44], cb_in[:, :], "cb")
    ld(lam_t[:, 0:256], lamv[:, :], "lam_t")
    ld(g_sub[:, 0:1], gsub[:, :], "g_sub")
    ld(g_sb[:, 0:1], gsb[:, :], "g_sb")
    ld(b31_t[:, 0:4], b31[:, :], "b31")
    ld(hv[:, 0:NSLOT], hv_in[:, :], "hv")
    ld(kcol[:, 0:NBLK], kcol_in[:, :], "kcol")

    P.op("dve", lambda e: e.tensor_tensor(out=lamtmp[:, 0:64], in0=lam_t[:, 0:64], in1=lam_t[:, 64:128], op=ALU.mult),
         reads=["lam_t"], writes=["lamtmp0"])
    P.op("dve", lambda e: e.tensor_tensor(out=lamtmp[:, 64:128], in0=lam_t[:, 128:192], in1=lam_t[:, 192:256], op=ALU.mult),
         reads=["lam_t"], writes=["lamtmp1"])
    P.op("dve", lambda e: e.reduce_sum(out=lamred[:, 0:1], in_=lamtmp[:, 0:64], axis=mybir.AxisListType.X),
         reads=["lamtmp0"], writes=["lamred0"])
    P.op("dve", lambda e: e.reduce_sum(out=lamred[:, 1:2], in_=lamtmp[:, 64:128], axis=mybir.AxisListType.X),
         reads=["lamtmp1"], writes=["lamred1"])
    P.op("act", lambda e: e.activation(out=lamred[:, 2:4], in_=lamred[:, 0:2], func=AF.Exp),
         reads=["lamred0", "lamred1"], writes=["lamexp"])
    P.op("dve", lambda e: e.scalar_tensor_tensor(out=neglam[:, 0:1], in0=lamred[:, 3:4], scalar=-LAM_INIT,
                                                 in1=lamred[:, 2:3], op0=ALU.add, op1=ALU.subtract),
         reads=["lamexp"], writes=["neglam"])
    P.op("dve", lambda e: e.tensor_scalar(out=gsub8[:, 0:1], in0=g_sub[:, 0:1], scalar1=(1.0 - LAM_INIT), scalar2=None,
                                          op0=ALU.mult), reads=["g_sub"], writes=["gsub8"])

    base_mark = AR.mark()

    Qd = v3(AR.alloc(8 * NQ * 2, BF), 8)
    P.op("pool", lambda e: e.memset(Qd[:, :, :], 0.0), writes=["Q"])
    Qs = v3(AR.alloc(4 * NQ * 2, BF), 4)
    p1_mark = AR.mark()
    wq = v3(AR.alloc(8 * 1024 * 2, BF), 8)
    wk = v3(AR.alloc(8 * 1024 * 2, BF), 8)
    wv = v3(AR.alloc(8 * 1024 * 2, BF), 8)
    wst = [v3(AR.alloc(8 * 256 * 4), 8) for _ in range(2)]

    wqkv_v = w_qkv.rearrange("(c p) n -> p c n", p=128)
    wplan = [(wq, 0, 0, 0.125), (wk, 0, 512, 1.0), (wv, 0, 1024, 1.0), (wq, 512, 1536, 0.125), (wk, 512, 2048, 1.0), (wv, 512, 2560, 1.0)]
    li = 0
    for (wdst, dcol, scol, scl) in wplan:
        for half in range(2):
            buf = wst[li % 2]
            c0 = scol + half * 256
            P.dma("sp", "wst", 2, lambda e, buf=buf, c0=c0: e.dma_start(out=buf[:, :, :], in_=wqkv_v[:, :, c0:c0 + 256]),
                  writes=[("wst", li % 2)])
            for fc in range(8):
                P.op("dve", lambda e, buf=buf, fc=fc, wdst=wdst, d0=dcol + half * 256, scl=scl: e.tensor_scalar(
                    out=wdst[:, fc, d0:d0 + 256], in0=buf[:, fc, :], scalar1=g_pre[:, fc:fc + 1], scalar2=scl, op0=ALU.mult, op1=ALU.mult),
                    reads=[("wst", li % 2), "g_pre"], writes=["wproj"])
            li += 1

    xt = [alloc_p1(1024 * 4) for _ in range(3)]
    xs = [alloc_p1(1024 * 2, BF) for _ in range(2)]
    junk = alloc_p1(1024 * 2, BF)
    ssq = AR.alloc(64 * 4)
    hT = [v3(AR.alloc(8 * 512 * 2, BF), 8) for _ in range(2)]
    kst = [v3(AR.alloc(8 * 512 * 2, BF), 8) for _ in range(2)]
    vstd = [alloc_p1(4 * 4 * 128 * 2, BF) for _ in range(2)]
    vsts = [AR.alloc(8 * 4 * 128 * 2, BF) for _ in range(2)]
    for i in range(2):
        P.op("pool", lambda e, i=i: e.memset(vsts[i][:, :], 0.0), writes=[("vsts", i)])

    cnt = {"x": 0, "h": 0, "ev": 0}

    def x_to_hT(src_rows_ap, n, hbuf_i, col0):
        i = cnt["x"]
        cnt["x"] += 1
        xb = xt[i % 3]
        xsb = xs[i % 2]
        sc = ssq[:, (i % 16) * 4:(i % 16) * 4 + 4]
        P.dma("sp", "xt", 3, lambda e: e.dma_start(out=xb[0:n, :], in_=src_rows_ap), writes=[("xt", i % 3)])
        P.op("act", lambda e: e.activation(out=junk[0:n, :], in_=xb[0:n, :], func=AF.Square, accum_out=sc[0:n, 0:1]),
             reads=[("xt", i % 3)], writes=["junk", ("ss", i % 16)])
        P.op("act", lambda e: e.activation(out=sc[0:n, 1:2], in_=sc[0:n, 0:1], func=AF.Sqrt, scale=1.0 / D, bias=eps_t[0:n, 0:1]),
             reads=[("ss", i % 16), "eps"], writes=[("sd", i % 16)])
        P.op("dve", lambda e: e.reciprocal(out=sc[0:n, 2:3], in_=sc[0:n, 1:2]), reads=[("sd", i % 16)], writes=[("rs", i % 16)])
        P.op("act", lambda e: e.activation(out=xsb[0:n, :], in_=xb[0:n, :], func=AF.Copy, scale=sc[0:n, 2:3]),
             reads=[("xt", i % 3), ("rs", i % 16)], writes=[("xs", i % 2)])
        pb = 6 + (i % 2)
        tp = v3(PSB[pb], 8)
        for f in range(8):
            P.op("pe", lambda e, f=f: e.transpose(out=tp[:, f, 0:n], in_=xsb[0:n, f * 128:(f + 1) * 128], identity=IDENT[0:n, 0:n]),
                 reads=[("xs", i % 2), "cm"], writes=[("ps", pb)])
        P.op("dve", lambda e: e.tensor_copy(out=hT[hbuf_i][:, :, col0:col0 + n], in_=tp[:, :, 0:n]),
             reads=[("ps", pb)], writes=[("hT", hbuf_i)])


    def evac(dst, src, reads, writes):
        i = cnt["ev"]
        cnt["ev"] += 1
        if i % 3 == 2:
            P.op("act", lambda e: e.activation(out=dst, in_=src, func=AF.Copy), reads=reads, writes=writes)
        else:
            P.op("dve", lambda e: e.tensor_copy(out=dst, in_=src), reads=reads, writes=writes)

    def q_slot(m, hi):
        r0 = 0
        while r0 < TC:
            n = min(128, TC - r0)
            x_to_hT(xq[m * TC + r0:m * TC + r0 + n, :], n, hi, r0)
            r0 += n
        for g in range(8):
            pb = g % 4
            for fc in range(8):
                P.op("pe", lambda e, fc=fc, g=g, pb=pb: e.matmul(
                    PSF[pb][:, 0:TC], lhsT=wq[:, fc, g * 128:(g + 1) * 128], rhs=hT[hi][:, fc, 0:TC],
                    start=(fc == 0), stop=(fc == 7)),
                    reads=[("hT", hi), "wproj"], writes=[("ps", pb)])
            if g < 4:
                evac(Qd[0:64, 2 * g, m * TC:(m + 1) * TC], PSF[pb][0:64, 0:TC], [("ps", pb)], ["Q"])
                evac(Qd[64:128, 2 * g + 1, m * TC:(m + 1) * TC], PSF[pb][64:128, 0:TC], [("ps", pb)], ["Q"])
            else:
                evac(Qs[:, g - 4, m * TC:(m + 1) * TC], PSF[pb][:, 0:TC], [("ps", pb)], ["Q"])

    for m in range(NSLOT):
        q_slot(m, cnt["h"] % 2)
        cnt["h"] += 1

    kt_v = kt_scr.rearrange("(c p) t -> p c t", p=128)
    vd_v = vd_scr.rearrange("u p (b d) -> p u b d", d=128)
    vs_v = vs_scr.rearrange("u p (b d) -> p u b d", d=128)
    NSUP = SEQ // 512

    def kv_super(s, hi):
        for r in range(4):
            x_to_hT(x_rev[s * 512 + r * 128:s * 512 + (r + 1) * 128, :], 128, hi, r * 128)
        ks = kst[s % 2]
        for dc in range(8):
            pb = dc % 4
            for fc in range(8):
                P.op("pe", lambda e, fc=fc, dc=dc, pb=pb: e.matmul(
                    PSF[pb][:, 0:512], lhsT=wk[:, fc, dc * 128:(dc + 1) * 128], rhs=hT[hi][:, fc, 0:512],
                    start=(fc == 0), stop=(fc == 7)),
                    reads=[("hT", hi), "wproj"], writes=[("ps", pb)])
            evac(ks[:, dc, :], PSF[pb][:, 0:512], [("ps", pb)], [("kst", s % 2)])
        P.dma("pool", "kst", 2, lambda e: e.dma_start(out=kt_v[:, :, s * 512:(s + 1) * 512], in_=ks[:, :, :]),
              reads=[("kst", s % 2)], writes=["kt_scr"])
        vd = vstd[s % 2].rearrange("p (u b d) -> p u b d", u=4, b=4)
        vs = vsts[s % 2].rearrange("p (u b d) -> p u b d", u=8, b=4)
        for r in range(4):
            for cg in range(2):
                pb = 4 + cg
                for fc in range(8):
                    P.op("pe", lambda e, fc=fc, cg=cg, pb=pb, r=r: e.matmul(
                        PSF[pb][:, 0:512], lhsT=hT[hi][:, fc, r * 128:(r + 1) * 128], rhs=wv[:, fc, cg * 512:(cg + 1) * 512],
                        start=(fc == 0), stop=(fc == 7)),
                        reads=[("hT", hi), "wproj"], writes=[("ps", pb)])
                if cg == 0:
                    evac(vd[:, :, r, :], PSF[pb][:, 0:512].rearrange("p (u d) -> p u d", u=4), [("ps", pb)], [("vstd", s % 2)])
                else:
                    src = PSF[pb][:, 0:512].rearrange("p (u d) -> p u d", u=8)
                    for par in range(2):
                        evac(vs[:, par::2, r, par * 64:par * 64 + 64], src[:, par::2, :], [("ps", pb)], [("vsts", s % 2)])
        P.dma("pool", "vst", 2, lambda e: e.dma_start(out=vd_v[:, :, s * 4:(s + 1) * 4, :], in_=vd),
              reads=[("vstd", s % 2)], writes=["vd_scr"])
        P.dma("pool", "vst2", 2, lambda e: e.dma_start(out=vs_v[:, :, s * 4:(s + 1) * 4, :], in_=vs),
              reads=[("vsts", s % 2)], writes=["vs_scr"])

    wup_v0 = w_up.rearrange("(c p) n -> p c n", p=128)
    wcast = []
    for j in range(NJ):
        for half in range(2):
            dstv = wup_scr[j].rearrange("p (c n) -> p c n", c=8)[:, :, half * 128:(half + 1) * 128]
            srcv = wup_v0[:, :, half * DFF + j * 128:half * DFF + (j + 1) * 128]
            wcast.append(("wc%d" % (len(wcast) % 4), dstv, srcv))
    for jj in range(0, NJ, 2):
        dstv = wd_scr.rearrange("p (c n) -> p c n", c=NJ)[:, jj:jj + 2, :]
        srcv = w_down.rearrange("(c p) n -> p c n", p=128)[:, jj:jj + 2, :]
        wcast.append(("wc%d" % (len(wcast) % 4), dstv, srcv))
    for cc in range(0, 8, 2):
        dstv = wo_scr.rearrange("p (c n) -> p c n", c=8)[:, cc:cc + 2, :]
        srcv = w_o.rearrange("(c p) n -> p c n", p=128)[:, cc:cc + 2, :]
        wcast.append(("wc%d" % (len(wcast) % 4), dstv, srcv))

    def issue_wcast(n):
        for _ in range(n):
            if wcast:
                nm, dstv, srcv = wcast.pop(0)
                P.dma("pool", nm, 1, lambda e, dstv=dstv, srcv=srcv: e.dma_start(out=dstv, in_=srcv), writes=["wscr"])

    per_sup = -(-len(wcast) // NSUP)
    for s_ in range(NSUP):
        kv_super(s_, cnt["h"] % 2)
        cnt["h"] += 1
        issue_wcast(per_sup)
    issue_wcast(len(wcast))

    P.barrier()

    AR.reset(p1_mark)
    KT = AR.alloc(SEQ * 2, BF)
    Vb = v3(AR.alloc(NBLK * 128 * 2, BF), NBLK)
    posrow = AR.alloc(NQ * 4)
    Eb = [AR.alloc(TC * 4) for _ in range(2)]
    Lb = [AR.alloc(TC * 2, BF) for _ in range(2)]
    Ls = [AR.alloc(TC * 2, BF) for _ in range(2)]
    Ab = [AR.alloc(TC * 2, BF) for _ in range(2)]
    Mk = [AR.alloc(TC * 2, BF) for _ in range(3)]
    NBT = 6
    Bt = [AR.alloc(TC * 4) for _ in range(NBT)]
    Tt = [AR.alloc(TC * 4) for _ in range(2)]
    Pb = [AR.alloc(TC * 2, BF) for _ in range(3)]
    Osb = [AR.alloc(TC * 4) for _ in range(4)]
    fin = [AR.alloc(TC * 4) for _ in range(4)]
    finb = AR.alloc(TC * 2, BF)
    lnt = AR.alloc(TC * 4)
    P.dma("sp", "posrow", 1, lambda e: e.dma_start(out=posrow[:, 0:NQ], in_=posrow_in[:, :]), writes=["posrow"])

    def cols(a, n=TC):
        return a[:, 0:n]

    steps_sb = []
    for m in range(NSLOT):
        for kb in range(cfg.NB[m] - 1, -1, -1):
            steps_sb.append((m, kb, kb == cfg.NB[m] - 1, kb == 0, cfg.sb_masked(m, kb)))

    def sb_head(h):
        hh = h % 2
        rs = slice(hh * 64, hh * 64 + 64)
        P.dma("sp", "kt", 1, lambda e: e.dma_start(out=KT[rs, :], in_=kt_scr[512 + 64 * h:512 + 64 * h + 64, :]),
              reads=["kt_scr"], writes=["KT"])
        P.dma("sp", "vb", 1, lambda e: e.dma_start(out=Vb[:, :, :], in_=vs_scr[h].rearrange("p (b d) -> p b d", d=128)),
              reads=["vs_scr"], writes=["V"])
        n = len(steps_sb)

        def st_M(k):
            m, kb, first, last, masked = steps_sb[k]
            if masked:
                P.op("dve", lambda e: e.tensor_scalar(out=cols(Mk[k % 3]), in0=posrow[:, m * TC:(m + 1) * TC],
                                                      scalar1=kcol[:, kb:kb + 1], scalar2=0.0, op0=ALU.subtract, op1=ALU.min),
                     reads=["posrow", "kcol"], writes=[("Mk", k % 3)])

        def warm(nw):
            for _ in range(nw):
                P.op("pe", lambda e: e.matmul(PSF[7][:, 0:512], lhsT=IDENT, rhs=KT[:, 0:512], start=True, stop=True),
                     reads=[], writes=[("ps", 7)])

        def st_Z(k):
            m, kb, first, last, masked = steps_sb[k]
            pb = k % 4
            P.op("pe", lambda e: e.matmul(PSF[pb][:, 0:TC], lhsT=KT[rs, kb * 128:(kb + 1) * 128], rhs=Qs[rs, h // 2, m * TC:(m + 1) * TC],
                                          start=True, stop=False),
                 reads=["KT", "Q"], writes=[("ps", pb)])
            if masked:
                P.op("pe", lambda e: e.matmul(PSF[pb][:, 0:TC], lhsT=IDENT, rhs=cols(Mk[k % 3]), start=False, stop=False),
                     reads=[("Mk", k % 3), "cm"], writes=[("ps", pb)])
            warm(WARM_SB)

        def st_E(k):
            P.op("act", lambda e: e.activation(out=cols(Eb[k % 2]), in_=PSF[k % 4][:, 0:TC], func=AF.Exp),
                 reads=[("ps", k % 4)], writes=[("E", k % 2)])

        def st_L(k):
            P.op("act", lambda e: e.activation(out=cols(Lb[k % 2]), in_=cols(Eb[k % 2]), func=AF.Ln, bias=one_t[:, 0:1], scale=1.0),
                 reads=[("E", k % 2), "one"], writes=[("L", k % 2)])

        def st_LS(k):
            m, kb, first, last, masked = steps_sb[k]
            if last:
                return
            if first:
                P.op("pool", lambda e: e.tensor_copy(out=cols(Ls[(k + 1) % 2]), in_=cols(Lb[k % 2])),
                     reads=[("L", k % 2)], writes=[("Ls", (k + 1) % 2)])
            else:
                P.op("pool", lambda e: e.tensor_tensor(out=cols(Ls[(k + 1) % 2]), in0=cols(Ls[k % 2]), in1=cols(Lb[k % 2]), op=ALU.add),
                     reads=[("L", k % 2), ("Ls", k % 2)], writes=[("Ls", (k + 1) % 2)])

        def st_ZC(k):
            m, kb, first, last, masked = steps_sb[k]
            pb = k % 4
            P.op("pe", lambda e: e.matmul(PSF[pb][:, 0:TC], lhsT=NEGTRI, rhs=cols(Lb[k % 2]), start=False, stop=first, skip_group_check=True),
                 reads=[("L", k % 2), "cm"], writes=[("ps", pb)])
            if not first:
                P.op("pe", lambda e: e.matmul(PSF[pb][:, 0:TC], lhsT=NEGONES, rhs=cols(Ls[k % 2]), start=False, stop=True, skip_group_check=True),
                     reads=[("Ls", k % 2), "cm"], writes=[("ps", pb)])

        def st_A(k):
            pb = k % 4
            P.op("act", lambda e: e.activation(out=cols(Ab[k % 2]), in_=PSF[pb][:, 0:TC], func=AF.Exp),
                 reads=[("ps", pb)], writes=[("A", k % 2)])

        def st_AV(k):
            m, kb, first, last, masked = steps_sb[k]
            ob = 4 + (m % 2)
            P.op("pe", lambda e: e.matmul(PSF[ob][:, 0:TC], lhsT=Vb[:, kb, :], rhs=cols(Ab[k % 2]), start=first, stop=last),
                 reads=["V", ("A", k % 2)], writes=[("ps", ob)])
            if last:
                finalize_sb(m, ob)

        def finalize_sb(m, ob):
            P.op("dve", lambda e: e.tensor_copy(out=cols(fin[0]), in_=PSF[ob][:, 0:TC]), reads=[("ps", ob)], writes=[("fin", 0)])
            P.op("dve", lambda e: e.tensor_tensor(out=cols(finb), in0=cols(fin[0]), in1=cols(fin[0]), op=ALU.mult),
                 reads=[("fin", 0)], writes=["finb"])
            P.op("pe", lambda e: e.matmul(PSF[6][:, 0:TC], lhsT=ONESBLK, rhs=cols(finb), start=True, stop=True),
                 reads=["finb", "cm"], writes=[("ps", 6)])
            P.op("act", lambda e: e.activation(out=cols(lnt), in_=PSF[6][:, 0:TC], func=AF.Ln, scale=1.0 / HD, bias=eps_t[:, 0:1]),
                 reads=[("ps", 6), "eps"], writes=["lnt"])
            P.op("act", lambda e: e.activation(out=cols(fin[1]), in_=cols(lnt), func=AF.Exp, scale=-0.5),
                 reads=["lnt"], writes=[("fin", 1)])
            P.op("dve", lambda e: e.scalar_tensor_tensor(out=mixT3[rs, 4 + h // 2, m * TC:(m + 1) * TC], in0=fin[0][rs, 0:TC],
                                                         scalar=g_sb[rs, 0:1], in1=fin[1][rs, 0:TC], op0=ALU.mult, op1=ALU.mult),
                 reads=[("fin", 0), ("fin", 1), "g_sb"], writes=["mixT"])

        stages = [(st_L, 2), (st_LS, 2), (st_ZC, 2), (st_AV, 4), (st_M, 0), (st_Z, 0), (st_E, 1), (st_A, 3)]
        for t in range(n + 4):
            for fn, lag in stages:
                k = t - lag
                if 0 <= k < n:
                    fn(k)

    for h_ in range(8):
        sb_head(h_)

    steps_df = []
    for m in range(NSLOT):
        for kb in range(cfg.NB[m]):
            steps_df.append((m, kb, kb == 0, kb == cfg.NB[m] - 1, cfg.df_masked(m, kb)))

    def df_head(h):
        P.dma("sp", "kt", 1, lambda e: e.dma_start(out=KT[:, :], in_=kt_scr[128 * h:128 * h + 128, :]),
              reads=["kt_scr"], writes=["KT"])
        P.dma("sp", "vb", 1, lambda e: e.dma_start(out=Vb[:, :, :], in_=vd_scr[h].rearrange("p (b d) -> p b d", d=128)),
              reads=["vd_scr"], writes=["V"])
        n = len(steps_df)
        n2 = 2 * n

        def sd_B(kk):
            k, mm = kk // 2, kk % 2
            m, kb, first, last, masked = steps_df[k]
            if masked and mm == 0:
                src = bass.AP(tensor=gtab.tensor, offset=h * cfg.TABLEN + cfg.base(m, kb), ap=[[1, 128], [1, TC]])
                P.dma("sp", "bt", NBT, lambda e: e.dma_start(out=cols(Bt[k % NBT]), in_=src), writes=[("Bt", k % NBT)])

        def sd_S(kk):
            k, mm = kk // 2, kk % 2
            m, kb, first, last, masked = steps_df[k]
            pb = kk % 2
            P.op("pe", lambda e: e.matmul(PSF[pb][:, 0:TC], lhsT=KT[:, kb * 128:(kb + 1) * 128],
                                          rhs=Qd[:, 2 * h + mm, m * TC:(m + 1) * TC], start=True, stop=True),
                 reads=["KT", "Q"], writes=[("ps", pb)])

        def sd_T(kk):
            k, mm = kk // 2, kk % 2
            m, kb, first, last, masked = steps_df[k]
            if masked:
                P.op("dve", lambda e: e.tensor_tensor(out=cols(Tt[kk % 2]), in0=PSF[kk % 2][:, 0:TC], in1=cols(Bt[k % NBT]), op=ALU.add),
                     reads=[("ps", kk % 2), ("Bt", k % NBT)], writes=[("Tt", kk % 2)])

        def sd_P(kk):
            k, mm = kk // 2, kk % 2
            m, kb, first, last, masked = steps_df[k]
            if masked:
                P.op("act", lambda e: e.activation(out=cols(Pb[kk % 3]), in_=cols(Tt[kk % 2]), func=AF.Exp),
                     reads=[("Tt", kk % 2)], writes=[("Pb", kk % 3)])
            else:
                P.op("act", lambda e: e.activation(out=cols(Pb[kk % 3]), in_=PSF[kk % 2][:, 0:TC], func=AF.Exp, bias=b31_t[:, h:h + 1], scale=1.0),
                     reads=[("ps", kk % 2), "b31"], writes=[("Pb", kk % 3)])

        def sd_PV(kk):
            k, mm = kk // 2, kk % 2
            m, kb, first, last, masked = steps_df[k]
            ob = 2 + mm
            lb = 4 + mm
            P.op("pe", lambda e: e.matmul(PSF[ob][:, 0:TC], lhsT=Vb[:, kb, :], rhs=cols(Pb[kk % 3]), start=first, stop=last),
                 reads=["V", ("Pb", kk % 3)], writes=[("ps", ob)])
            P.op("pe", lambda e: e.matmul(PSF[lb][:, 0:TC], lhsT=ONES, rhs=cols(Pb[kk % 3]), start=first, stop=last),
                 reads=[("Pb", kk % 3), "cm"], writes=[("ps", lb)])
            for _ in range(WARM_DF):
                P.op("pe", lambda e: e.matmul(PSF[7][:, 0:512], lhsT=IDENT, rhs=KT[:, 0:512], start=True, stop=True),
                     reads=[], writes=[("ps", 7)])
            if last and mm == 1:
                finalize_df(m)

        def finalize_df(m):
            for q in range(2):
                P.op("dve", lambda e, q=q: e.tensor_copy(out=cols(Osb[q]), in_=PSF[2 + q][:, 0:TC]), reads=[("ps", 2 + q)], writes=[("Osb", q)])
                P.op("dve", lambda e, q=q: e.tensor_scalar(out=cols(Osb[2 + q]), in0=PSF[4 + q][:, 0:TC], scalar1=1e-30, scalar2=None, op0=ALU.max),
                     reads=[("ps", 4 + q)], writes=[("Osb", 2 + q)])
            for q in range(2):
                P.op("dve", lambda e, q=q: e.reciprocal(out=cols(fin[2 + q]), in_=cols(Osb[2 + q])), reads=[("Osb", 2 + q)], writes=[("fin", 2 + q)])
                P.op("dve", lambda e, q=q: e.tensor_tensor(out=cols(fin[2 + q]), in0=cols(Osb[q]), in1=cols(fin[2 + q]), op=ALU.mult),
                     reads=[("Osb", q), ("fin", 2 + q)], writes=[("fin", 2 + q)])
            P.op("dve", lambda e: e.scalar_tensor_tensor(out=cols(fin[0]), in0=cols(fin[3]), scalar=neglam[:, 0:1], in1=cols(fin[2]),
                                                         op0=ALU.mult, op1=ALU.add),
                 reads=[("fin", 2), ("fin", 3), "neglam"], writes=[("fin", 0)])
            P.op("dve", lambda e: e.tensor_tensor(out=cols(finb), in0=cols(fin[0]), in1=cols(fin[0]), op=ALU.mult),
                 reads=[("fin", 0)], writes=["finb"])
            P.op("pe", lambda e: e.matmul(PSF[6][:, 0:TC], lhsT=ONES, rhs=cols(finb), start=True, stop=True),
                 reads=["finb", "cm"], writes=[("ps", 6)])
            P.op("act", lambda e: e.activation(out=cols(lnt), in_=PSF[6][:, 0:TC], func=AF.Ln, scale=1.0 / 128.0, bias=eps_t[:, 0:1]),
                 reads=[("ps", 6), "eps"], writes=["lnt"])
            P.op("act", lambda e: e.activation(out=cols(fin[1]), in_=cols(lnt), func=AF.Exp, scale=-0.5),
                 reads=["lnt"], writes=[("fin", 1)])
            P.op("dve", lambda e: e.scalar_tensor_tensor(out=mixT3[:, h, m * TC:(m + 1) * TC], in0=cols(fin[0]),
                                                         scalar=gsub8[:, 0:1], in1=cols(fin[1]), op0=ALU.mult, op1=ALU.mult),
                 reads=[("fin", 0), ("fin", 1), "gsub8"], writes=["mixT"])

        stages = [(sd_PV, 3), (sd_B, 0), (sd_S, 0), (sd_T, 1), (sd_P, 1)]
        for t in range(n2 + 3):
            for fn, lag in stages:
                kk = t - lag
                if 0 <= kk < n2:
                    fn(kk)

    for h_ in range(4):
        df_head(h_)

    P.barrier()

    AR.reset(base_mark)
    wo = v3(AR.alloc(8 * 1024 * 2, BF), 8)
    wd = v3(AR.alloc(NJ * 1024 * 2, BF), NJ)
    NWU = 3
    wu = [v3(AR.alloc(8 * 256 * 2, BF), 8) for _ in range(NWU)]
    gvT = v3(AR.alloc(NJ * TC * 2, BF), NJ)
    xT = v3(AR.alloc(8 * TC * 4), 8)
    Mo = v3(AR.alloc(8 * TC * 4), 8)
    sqb = v3(AR.alloc(8 * TC * 2, BF), 8)
    x1T = v3(AR.alloc(8 * TC * 4), 8)
    h2T = v3(AR.alloc(8 * TC * 2, BF), 8)
    rstd = AR.alloc(TC * 4)
    tmp3 = AR.alloc(TC * 4)
    yg = [AR.alloc(TC * 4) for _ in range(2)]
    yv = [AR.alloc(TC * 4) for _ in range(2)]
    gl = [AR.alloc(TC * 4) for _ in range(2)]
    xrow = [AR.alloc(1024 * 4) for _ in range(2)]

    P.dma("sp", "wo", 1, lambda e: e.dma_start(out=wo[:, :, :], in_=wo_scr.rearrange("p (c n) -> p c n", c=8)), writes=["wo"])
    P.dma("sp", "wd", 1, lambda e: e.dma_start(out=wd[:, :, :], in_=wd_scr.rearrange("p (c n) -> p c n", c=NJ)), writes=["wd"])
    RB = 1

    def rms_rstd(src3, nfeat, tagname):
        for dc in range(8):
            P.op("pool", lambda e, dc=dc: e.tensor_tensor(out=sqb[:, dc, :], in0=src3[:, dc, :], in1=src3[:, dc, :], op=ALU.mult),
                 reads=[tagname], writes=["sqb"])
        for dc in range(8):
            P.op("pe", lambda e, dc=dc: e.matmul(PSF[RB][:, 0:TC], lhsT=ONES, rhs=sqb[:, dc, :], start=(dc == 0), stop=(dc == 7)),
                 reads=["sqb", "cm"], writes=[("ps", RB)])
        P.op("act", lambda e: e.activation(out=cols(tmp3), in_=PSF[RB][:, 0:TC], func=AF.Ln, scale=1.0 / nfeat, bias=eps_t[:, 0:1]),
             reads=[("ps", RB), "eps"], writes=["tmp3"])
        P.op("act", lambda e: e.activation(out=cols(rstd), in_=cols(tmp3), func=AF.Exp, scale=-0.5), reads=["tmp3"], writes=["rstd"])

    jcount = {"n": 0, "row": 0}

    def ffn_j(m, j):
        ji = jcount["n"]
        jcount["n"] += 1
        wb = wu[ji % NWU]
        P.dma("sp", "wu", NWU, lambda e: e.dma_start(out=wb[:, :, :], in_=wup_scr[j].rearrange("p (c n) -> p c n", c=8)),
              writes=[("wu", ji % NWU)])
        for half in range(2):
            pb = 4 + 2 * (ji % 2) + half
            for fc in range(8):
                P.op("pe", lambda e, fc=fc, half=half, pb=pb: e.matmul(
                    PSF[pb][:, 0:TC], lhsT=wb[:, fc, half * 128:(half + 1) * 128], rhs=h2T[:, fc, :], start=(fc == 0), stop=(fc == 7)),
                    reads=[("wu", ji % NWU), "h2T"], writes=[("ps", pb)])
            ydst = (yg if half == 0 else yv)[ji % 2]
            ch = j + half * NJ
            yname = ("y", half, ji % 2)
            P.op("dve", lambda e, ydst=ydst, pb=pb, ch=ch: e.tensor_scalar(
                out=cols(ydst), in0=PSF[pb][:, 0:TC], scalar1=cw[:, ch * 3 + 2:ch * 3 + 3], scalar2=cb[:, ch:ch + 1], op0=ALU.mult, op1=ALU.add),
                reads=[("ps", pb), "cw", "cb"], writes=[yname])
            P.op("dve", lambda e, ydst=ydst, pb=pb, ch=ch: e.scalar_tensor_tensor(
                out=ydst[:, 1:TC], in0=PSF[pb][:, 0:TC - 1], scalar=cw[:, ch * 3 + 1:ch * 3 + 2], in1=ydst[:, 1:TC], op0=ALU.mult, op1=ALU.add),
                reads=[("ps", pb), "cw", yname], writes=[yname])
            P.op("dve", lambda e, ydst=ydst, pb=pb, ch=ch: e.scalar_tensor_tensor(
                out=ydst[:, 2:TC], in0=PSF[pb][:, 0:TC - 2], scalar=cw[:, ch * 3 + 0:ch * 3 + 1], in1=ydst[:, 2:TC], op0=ALU.mult, op1=ALU.add),
                reads=[("ps", pb), "cw", yname], writes=[yname])
        P.op("act", lambda e: e.activation(out=cols(gl[ji % 2]), in_=cols(yg[ji % 2]), func=AF.Gelu_apprx_tanh),
             reads=[("y", 0, ji % 2)], writes=[("gl", ji % 2)])
        P.op("pool", lambda e: e.tensor_tensor(out=gvT[:, j, :], in0=cols(gl[ji % 2]), in1=cols(yv[ji % 2]), op=ALU.mult),
             reads=[("gl", ji % 2), ("y", 1, ji % 2)], writes=["gvT"])

    def p3_slot(m):
        r0 = 0
        while r0 < TC:
            n = min(128, TC - r0)
            ri = jcount["row"]
            jcount["row"] += 1
            xb = xrow[ri % 2]
            P.dma("sp", "xrow", 2, lambda e, xb=xb, n=n, r0=r0: e.dma_start(out=xb[0:n, :], in_=xq[m * TC + r0:m * TC + r0 + n, :]),
                  writes=[("xrow", ri % 2)])
            for half in range(2):
                pb = 2 + half
                for f4 in range(4):
                    f = half * 4 + f4
                    P.op("pe", lambda e, xb=xb, n=n, f=f, f4=f4, pb=pb: e.transpose(
                        out=PSF[pb][:, f4 * 128:f4 * 128 + n], in_=xb[0:n, f * 128:(f + 1) * 128], identity=IDENTF[0:n, 0:n]),
                        reads=[("xrow", ri % 2), "identf"], writes=[("ps", pb)])
                P.op("dve", lambda e, n=n, r0=r0, half=half, pb=pb: e.tensor_copy(
                    out=xT[:, half * 4:half * 4 + 4, r0:r0 + n], in_=PSF[pb][:, :].rearrange("p (a b) -> p a b", a=4)[:, :, 0:n]),
                    reads=[("ps", pb)], writes=["xT"])
            r0 += n
        for dc in range(8):
            pb = 2 + dc % 2
            for fc in range(8):
                P.op("pe", lambda e, dc=dc, fc=fc, pb=pb: e.matmul(PSF[pb][:, 0:TC], lhsT=wo[:, fc, dc * 128:(dc + 1) * 128],
                                                                   rhs=mixT3[:, fc, m * TC:(m + 1) * TC], start=(fc == 0), stop=(fc == 7)),
                     reads=["wo", "mixT"], writes=[("ps", pb)])
            P.op("dve", lambda e, dc=dc, pb=pb: e.tensor_copy(out=Mo[:, dc, :], in_=PSF[pb][:, 0:TC]), reads=[("ps", pb)], writes=["Mo"])
        rms_rstd(Mo, D, "Mo")
        for dc in range(8):
            P.op("dve", lambda e, dc=dc: e.scalar_tensor_tensor(out=Mo[:, dc, :], in0=Mo[:, dc, :], scalar=g_post[:, dc:dc + 1], in1=cols(rstd),
                                                                op0=ALU.mult, op1=ALU.mult), reads=["Mo", "rstd", "g_post"], writes=["Mo"])
            P.op("dve", lambda e, dc=dc: e.tensor_tensor(out=x1T[:, dc, :], in0=Mo[:, dc, :], in1=xT[:, dc, :], op=ALU.add),
                 reads=["Mo", "xT"], writes=["x1T"])
        rms_rstd(x1T, D, "x1T")
        for dc in range(8):
            P.op("dve", lambda e, dc=dc: e.scalar_tensor_tensor(out=h2T[:, dc, :], in0=x1T[:, dc, :], scalar=g_ffn[:, dc:dc + 1], in1=cols(rstd),
                                                                op0=ALU.mult, op1=ALU.mult), reads=["x1T", "rstd", "g_ffn"], writes=["h2T"])
        P.op("dve", lambda e: e.tensor_scalar(out=h2T[:, :, 0:2], in0=h2T[:, :, 0:2], scalar1=hv[:, m:m + 1], scalar2=None, op0=ALU.mult),
             reads=["h2T", "hv"], writes=["h2T"])
        for j in range(NJ):
            ffn_j(m, j)
        for dc in range(8):
            pb = 2 + dc % 2
            for j in range(NJ):
                P.op("pe", lambda e, dc=dc, j=j, pb=pb: e.matmul(PSF[pb][:, 0:TC], lhsT=wd[:, j, dc * 128:(dc + 1) * 128], rhs=gvT[:, j, :],
                                                                 start=(j == 0), stop=(j == NJ - 1)),
                     reads=["wd", "gvT"], writes=[("ps", pb)])
            P.op("dve", lambda e, dc=dc, pb=pb: e.tensor_copy(out=Mo[:, dc, :], in_=PSF[pb][:, 0:TC]), reads=[("ps", pb)], writes=["Mo"])
        rms_rstd(Mo, D, "Mo")
        for dc in range(8):
            P.op("dve", lambda e, dc=dc: e.scalar_tensor_tensor(out=Mo[:, dc, :], in0=Mo[:, dc, :], scalar=g_post2[:, dc:dc + 1], in1=cols(rstd),
                                                                op0=ALU.mult, op1=ALU.mult), reads=["Mo", "rstd", "g_post2"], writes=["Mo"])
            P.op("dve", lambda e, dc=dc: e.tensor_tensor(out=xT[:, dc, :], in0=Mo[:, dc, :], in1=x1T[:, dc, :], op=ALU.add),
                 reads=["Mo", "x1T"], writes=["xT"])
        r0 = 0
        while r0 < TC:
            n = min(128, TC - r0)
            ri = jcount["row"]
            jcount["row"] += 1
            ob_ = xrow[ri % 2]
            for half in range(2):
                pb = 2 + half
                for f4 in range(4):
                    f = half * 4 + f4
                    P.op("pe", lambda e, n=n, r0=r0, f=f, f4=f4, pb=pb: e.transpose(
                        out=PSF[pb][0:n, f4 * 128:(f4 + 1) * 128], in_=xT[:, f, r0:r0 + n], identity=IDENTF[:, :]),
                        reads=["xT", "identf"], writes=[("ps", pb)])
                P.op("dve", lambda e, n=n, half=half, pb=pb, ob_=ob_: e.tensor_copy(out=ob_[0:n, half * 512:(half + 1) * 512], in_=PSF[pb][0:n, 0:512]),
                     reads=[("ps", pb)], writes=[("xrow", ri % 2)])
            o = P.dma("sp", "xrow", 2, lambda e, n=n, r0=r0, ob_=ob_: e.dma_start(out=out[m * TC + r0:m * TC + r0 + n, :], in_=ob_[0:n, :]),
                      reads=[("xrow", ri % 2)], writes=["out"])
            P.out_dmas.append(o)
            r0 += n

    for m_ in range(NSLOT):
        p3_slot(m_)

    if debug:
        o = P.dma("sp", "dbg", 1, lambda e: e.dma_start(out=dbg[:, :], in_=mixT[:, :]), reads=["mixT"], writes=["dbg"])
        P.out_dmas.append(o)

    P.emit()
    st.close()
    return nc


def _bucket(n):
    n = np.maximum(n, 0)
    nf = np.maximum(n, 1).astype(np.float32)
    large = 16 + (np.log(nf / np.float32(16)) / np.float32(math.log(128 / 16)) * np.float32(16)).astype(np.int32)
    large = np.minimum(large, 31)
    return np.where(n < 16, n, large)


def make_inputs(cfg, inp):
    SEQ, NSLOT, W, TC, NBLK, NQ = cfg.SEQ, cfg.NSLOT, cfg.W, cfg.TC, cfg.NBLK, cfg.NQ
    f32 = np.float32
    x = np.ascontiguousarray(np.asarray(inp["x"], f32)[0])
    x_rev = np.ascontiguousarray(x.reshape(NBLK, 128, D)[:, ::-1, :].reshape(SEQ, D))

    def pc(v, c):
        return np.ascontiguousarray(np.asarray(v, f32).reshape(c, 128).T)

    shared = {
        "x_rev": x_rev,
        "w_qkv": np.ascontiguousarray(np.asarray(inp["w_qkv"], f32)[0]),
        "w_o": np.ascontiguousarray(np.asarray(inp["w_o"], f32)[0]),
        "w_up": np.ascontiguousarray(np.asarray(inp["w_up"], f32)[0]),
        "w_down": np.ascontiguousarray(np.asarray(inp["w_down"], f32)[0]),
        "gpre": pc(inp["attn_pre_norm"][0], 8),
        "gpost": pc(inp["attn_post_norm"][0], 8),
        "gffn": pc(inp["ffn_pre_norm"][0], 8),
        "gpost2": pc(inp["ffn_post_norm"][0], 8),
        "cw": np.ascontiguousarray(np.asarray(inp["conv_w"], f32)[0].reshape(3, 44, 128).transpose(2, 1, 0).reshape(128, 132)),
        "cb": pc(inp["conv_b"][0], 44),
        "lamv": np.ascontiguousarray(np.broadcast_to(np.concatenate(
            [np.asarray(inp[k], f32)[0] for k in ("lambda_q1", "lambda_k1", "lambda_q2", "lambda_k2")])[None, :], (128, 256))),
        "gsub": np.ascontiguousarray(np.asarray(inp["diff_subln"], f32)[0].reshape(128, 1)),
        "gsb": np.ascontiguousarray(np.tile(np.asarray(inp["sb_norm"], f32)[0], 2).reshape(128, 1)),
        "b31": np.ascontiguousarray(np.broadcast_to(np.asarray(inp["rel_bias"], f32)[31][None, :], (128, 4))),
        "identf": np.eye(128, dtype=f32),
    }
    pj = np.arange(128)
    ident = np.eye(128, dtype=f32)
    negtri = -(pj[:, None] <= pj[None, :]).astype(f32)
    negones = -np.ones((128, 128), f32)
    ones = np.ones((128, 128), f32)
    onesblk = ((pj[:, None] // 64) == (pj[None, :] // 64)).astype(f32)
    shared["cmats"] = np.concatenate([ident, negtri, negones, ones, onesblk], axis=1).astype(ml_dtypes.bfloat16)
    kb = np.arange(NBLK)
    shared["kcol"] = np.ascontiguousarray(((128 * kb[None, :] + 127 - pj[:, None]) * BIG).astype(f32))

    rel_bias = np.asarray(inp["rel_bias"], f32)
    maps = []
    for c in range(NCORES):
        mp = dict(shared)
        idx = np.zeros(NQ, np.int64)
        pos = np.zeros(NQ, np.int64)
        hvv = np.ones((128, NSLOT), f32)
        for m in range(NSLOT):
            t = cfg.t0(c, m) + np.arange(TC)
            pos[m * TC:(m + 1) * TC] = t
            idx[m * TC:(m + 1) * TC] = np.clip(t, 0, SEQ - 1)
            if t[0] < 0:
                hvv[:, m] = 0.0
        mp["xq"] = np.ascontiguousarray(x[idx])
        mp["hv"] = hvv
        mp["posrow"] = np.ascontiguousarray(np.broadcast_to(((pos - 1) * BIG).astype(f32)[None, :], (128, NQ)))
        n = np.arange(cfg.TABLEN)
        d = n - cfg.OFF0 + W * c - 2 - 127
        tab = np.where(d[None, :] >= 0, rel_bias[_bucket(d), :].T, f32(NEG)).astype(f32)
        mp["gtab"] = np.ascontiguousarray(tab)
        maps.append(mp)
    return maps


def assemble(cfg, results):
    SEQ, NSLOT, W, TC = cfg.SEQ, cfg.NSLOT, cfg.W, cfg.TC
    outp = np.zeros((SEQ, D), np.float32)
    for c in range(NCORES):
        o = results[c]["out"]
        for m in range(NSLOT):
            t0 = W * (8 * m + c)
            n = min(W, SEQ - t0)
            if n <= 0:
                continue
            outp[t0:t0 + n] = o[m * TC + 2:m * TC + 2 + n]
    return outp[None]


_NC_CACHE = {}


def run(cfg, inp, debug=None, trace=False):
    key = (cfg.SEQ, cfg.NSLOT, cfg.W, debug)
    if key not in _NC_CACHE:
        _NC_CACHE[key] = build(cfg, debug)
    nc = _NC_CACHE[key]
    maps = make_inputs(cfg, inp)
    res = run_bass_kernel_spmd(nc, maps, core_ids=list(range(NCORES)), **({"trace": True} if trace else {}))
    return res


def kernel(**inputs):
    res = run(FULL, inputs)
    return assemble(FULL, res.results)
```

```python
import math
import numpy as np
import ml_dtypes
import concourse.bass as bass
import concourse.mybir as mybir
from concourse.bass_utils import run_bass_kernel_spmd

F32 = mybir.dt.float32
BF = mybir.dt.bfloat16
AF = mybir.ActivationFunctionType
ALU = mybir.AluOpType

NCORES = 8
WARM_SB = 0
WARM_DF = 0
D = 1024
HD = 64
EPS = 1e-6
LAM_INIT = 0.8 - 0.6 * math.exp(-0.3 * 0)
BIG = 32768.0
NEG = -30000.0
DFF = 2816
NJ = DFF // 128


class Cfg:
    def __init__(self, seq, nslot, w):
        self.SEQ = seq
        self.NSLOT = nslot
        self.W = w
        self.TC = w + 2
        self.NBLK = seq // 128
        self.NQ = nslot * self.TC
        assert 8 * nslot * w >= seq
        self.NB = [min(self.NBLK, -(-(8 * (m + 1) * w) // 128)) for m in range(nslot)]
        self.OFF0 = max(128 * (self.NB[m] - 1) - 8 * w * m for m in range(nslot)) + 160
        self.TABLEN = max(8 * w * m + self.OFF0 for m in range(nslot)) + self.TC + 256

    def t0(self, c, m):
        return self.W * (8 * m + c) - 2

    def sb_masked(self, m, kb):
        return (self.t0(0, m) - 1) - (128 * kb + 127) < 0

    def df_masked(self, m, kb):
        return self.t0(0, m) - (128 * kb + 127) < 113

    def base(self, m, kb):
        return 8 * self.W * m - 128 * kb + self.OFF0


FULL = Cfg(16384, 5, 410)


class Op:
    __slots__ = ("eng", "fn", "deps", "sig", "need", "dma", "idx", "tag")

    def __init__(self, eng, fn, dma=None, tag=""):
        self.eng = eng
        self.fn = fn
        self.deps = []
        self.sig = None
        self.need = False
        self.dma = dma
        self.tag = tag


class Prog:
    ENGS = ("pe", "act", "dve", "pool", "sp")

    def __init__(self, nc):
        self.nc = nc
        self.ops = {e: [] for e in self.ENGS}
        self.lastw = {}
        self.readers = {}
        self.dma_count = {}
        self.dma_depth = {}
        self.all_dma_since_barrier = []
        self.out_dmas = []

    def _add(self, op, reads, writes):
        deps = []
        for r in reads:
            w = self.lastw.get(r)
            if w is not None:
                deps.append(w)
        for w_ in writes:
            w = self.lastw.get(w_)
            if w is not None:
                deps.append(w)
            deps.extend(self.readers.get(w_, ()))
        seen = set()
        for d in deps:
            if d is op or id(d) in seen:
                continue
            seen.add(id(d))
            if d.eng == "pe" and op.eng == "pe" and d.dma is None and op.dma is None:
                continue
            op.deps.append(d)
            d.need = True
        for r in reads:
            self.readers.setdefault(r, []).append(op)
        for w_ in writes:
            self.lastw[w_] = op
            self.readers[w_] = []
        self.ops[op.eng].append(op)
        return op

    def op(self, eng, fn, reads=(), writes=(), tag=""):
        return self._add(Op(eng, fn, tag=tag), reads, writes)

    def dma(self, eng, stream, depth, fn, reads=(), writes=(), tag=""):
        n = self.dma_count.get(stream, 0)
        self.dma_count[stream] = n + 1
        self.dma_depth[stream] = depth
        o = Op(eng, fn, dma=(stream, n % depth, 16 * (n // depth + 1)), tag=tag)
        o.need = True
        self.all_dma_since_barrier.append(o)
        return self._add(o, reads, writes)

    def barrier(self):
        lasts = [self.ops[e][-1] for e in self.ENGS if self.ops[e]]
        dmas = list(self.all_dma_since_barrier)
        self.all_dma_since_barrier = []
        for e in self.ENGS:
            o = Op(e, None, tag="barrier")
            for d in lasts + dmas:
                if d.eng == e and d.dma is None:
                    continue
                o.deps.append(d)
                d.need = True
            self.ops[e].append(o)
        self.lastw = {}
        self.readers = {}

    def emit(self):
        nc = self.nc
        import contextlib
        with contextlib.ExitStack() as st:
            esem = {e: st.enter_context(nc.semaphore("sem_" + e)) for e in ("pe", "act", "dve", "pool")}
            dsem = {}
            for s, dep in self.dma_depth.items():
                for i in range(dep):
                    dsem[(s, i)] = st.enter_context(nc.semaphore("d_%s_%d" % (s, i)))
            cnt = {e: 0 for e in esem}
            for e in self.ENGS:
                for o in self.ops[e]:
                    if o.dma is not None:
                        o.sig = (dsem[(o.dma[0], o.dma[1])], o.dma[2])
                    elif o.need and o.fn is not None:
                        cnt[e] += 1
                        o.sig = (esem[e], cnt[e])
                    elif o.need and o.fn is None:
                        o.sig = None
            block = st.enter_context(nc.Block())
            prog = self

            def run(eng_name, eng):
                waited = {}
                for o in prog.ops[eng_name]:
                    need = {}
                    for d in o.deps:
                        if d.sig is None:
                            continue
                        sem, val = d.sig
                        k = id(sem)
                        if k not in need or need[k][1] < val:
                            need[k] = (sem, val)
                    for k, (sem, val) in need.items():
                        if waited.get(k, 0) >= val:
                            continue
                        waited[k] = val
                        eng.wait_ge(sem, val)
                    if o.fn is None:
                        continue
                    ins = o.fn(eng)
                    if o.sig is not None:
                        if o.dma is not None:
                            ins.then_inc(o.sig[0], 16)
                        else:
                            ins.then_inc(o.sig[0], 1)
                if eng_name == "sp":
                    for o in prog.out_dmas:
                        sem, val = o.sig
                        if waited.get(id(sem), 0) < val:
                            waited[id(sem)] = val
                            eng.wait_ge(sem, val)

            @block.tensor
            def _(e):
                run("pe", e)

            @block.scalar
            def _(e):
                run("act", e)

            @block.vector
            def _(e):
                run("dve", e)

            @block.gpsimd
            def _(e):
                run("pool", e)

            @block.sync
            def _(e):
                run("sp", e)


class Arena:
    def __init__(self, ap_f32, nbytes):
        self.ap = ap_f32
        self.n = nbytes
        self.off = 0
        self.marks = []

    def alloc(self, nbytes, dtype=F32):
        assert nbytes % 4 == 0
        rounded = (nbytes + 31) // 32 * 32
        assert self.off + rounded <= self.n, ("arena overflow", self.off, rounded, self.n)
        a = self.ap[:, self.off // 4:(self.off + nbytes) // 4]
        self.off += rounded
        if dtype == BF:
            a = a.bitcast(BF)
        return a

    def mark(self):
        return self.off

    def reset(self, m):
        self.off = m


def v3(ap, a):
    return ap.rearrange("p (a b) -> p a b", a=a)


def build(cfg, debug=None):
    SEQ, NSLOT, W, TC, NBLK, NQ = cfg.SEQ, cfg.NSLOT, cfg.W, cfg.TC, cfg.NBLK, cfg.NQ
    nc = bass.Bass("TRN2", target_bir_lowering=False)

    def din(name, shape, dt=F32):
        return nc.dram_tensor(name, list(shape), dt, kind="ExternalInput").ap()

    x_rev = din("x_rev", [SEQ, D])
    xq = din("xq", [NQ, D])
    w_qkv = din("w_qkv", [D, 3072])
    w_o = din("w_o", [D, D])
    w_up = din("w_up", [D, 2 * DFF])
    w_down = din("w_down", [DFF, D])
    gpre = din("gpre", [128, 8])
    gpost = din("gpost", [128, 8])
    gffn = din("gffn", [128, 8])
    gpost2 = din("gpost2", [128, 8])
    cw_in = din("cw", [128, 44 * 3])
    cb_in = din("cb", [128, 44])
    lamv = din("lamv", [128, 4 * 64])
    gsub = din("gsub", [128, 1])
    gsb = din("gsb", [128, 1])
    b31 = din("b31", [128, 4])
    hv_in = din("hv", [128, NSLOT])
    posrow_in = din("posrow", [128, NQ])
    kcol_in = din("kcol", [128, NBLK])
    gtab = din("gtab", [4, cfg.TABLEN])
    consts_in = din("cmats", [128, 5 * 128], BF)
    identf_in = din("identf", [128, 128])
    out = nc.dram_tensor("out", [NQ, D], F32, kind="ExternalOutput").ap()
    if debug:
        dbg = nc.dram_tensor("dbg", [128, 8 * NQ], BF, kind="ExternalOutput").ap()

    kt_scr = nc.dram_tensor("kt_scr", [D, SEQ], BF).ap()
    vd_scr = nc.dram_tensor("vd_scr", [4, 128, NBLK * 128], BF).ap()
    vs_scr = nc.dram_tensor("vs_scr", [8, 128, NBLK * 128], BF).ap()
    wup_scr = nc.dram_tensor("wup_scr", [NJ, 128, 8 * 256], BF).ap()
    wd_scr = nc.dram_tensor("wd_scr", [128, NJ * 1024], BF).ap()
    wo_scr = nc.dram_tensor("wo_scr", [128, 8 * 1024], BF).ap()

    import contextlib
    st = contextlib.ExitStack()
    ARENA_BYTES = 204 * 1024
    arena_t = st.enter_context(nc.sbuf_tensor("arena", [128, ARENA_BYTES // 4], F32))
    AR = Arena(arena_t[:, :], ARENA_BYTES)
    banks = [st.enter_context(nc.psum_tensor("bank%d" % i, [128, 512], F32)) for i in range(8)]
    PSF = [b[:, :] for b in banks]
    PSB = [b[:, :].bitcast(BF) for b in banks]

    P = Prog(nc)

    cm = AR.alloc(5 * 128 * 2, BF)
    IDENT = cm[:, 0:128]
    NEGTRI = cm[:, 128:256]
    NEGONES = cm[:, 256:384]
    ONES = cm[:, 384:512]
    ONESBLK = cm[:, 512:640]
    IDENTF = AR.alloc(128 * 4)
    g_pre = AR.alloc(8 * 4)
    g_post = AR.alloc(8 * 4)
    g_ffn = AR.alloc(8 * 4)
    g_post2 = AR.alloc(8 * 4)
    cw = AR.alloc(44 * 3 * 4)
    cb = AR.alloc(44 * 4)
    lam_t = AR.alloc(4 * 64 * 4)
    g_sub = AR.alloc(32)
    g_sb = AR.alloc(32)
    b31_t = AR.alloc(32)
    hv = AR.alloc(max(32, NSLOT * 4))
    neglam = AR.alloc(32)
    lamtmp = AR.alloc(4 * 64 * 4)
    lamred = AR.alloc(32)
    gsub8 = AR.alloc(32)
    kcol = AR.alloc(NBLK * 4)
    eps_t = AR.alloc(32)
    one_t = AR.alloc(32)
    P.op("pool", lambda e: e.memset(eps_t[:, 0:1], EPS), writes=["eps"])
    P.op("pool", lambda e: e.memset(one_t[:, 0:1], 1.0), writes=["one"])
    mix_region = AR.alloc(8 * NQ * 2)
    mixT = mix_region.bitcast(BF)
    mixT3 = v3(mixT, 8)
    AR2 = Arena(mix_region, 8 * NQ * 2)

    def alloc_p1(nbytes, dtype=F32):
        rounded = (nbytes + 31) // 32 * 32
        if AR2.off + rounded <= AR2.n:
            return AR2.alloc(nbytes, dtype)
        return AR.alloc(nbytes, dtype)

    def ld(dst, src, name):
        P.dma("sp", "const", 16, lambda e, d=dst, s=src: e.dma_start(out=d, in_=s), writes=[name])

    ld(cm, consts_in[:, :], "cm")
    ld(IDENTF, identf_in[:, :], "identf")
    ld(g_pre[:, 0:8], gpre[:, :], "g_pre")
    ld(g_post[:, 0:8], gpost[:, :], "g_post")
    ld(g_ffn[:, 0:8], gffn[:, :], "g_ffn")
    ld(g_post2[:, 0:8], gpost2[:, :], "g_post2")
    ld(cw[:, 0:132], cw_in[:, :], "cw")
    ld(cb[:, 0:44], cb_in[:, :], "cb")
    ld(lam_t[:, 0:256], lamv[:, :], "lam_t")
    ld(g_sub[:, 0:1], gsub[:, :], "g_sub")
    ld(g_sb[:, 0:1], gsb[:, :], "g_sb")
    ld(b31_t[:, 0:4], b31[:, :], "b31")
    ld(hv[:, 0:NSLOT], hv_in[:, :], "hv")
    ld(kcol[:, 0:NBLK], kcol_in[:, :], "kcol")

    P.op("dve", lambda e: e.tensor_tensor(out=lamtmp[:, 0:64], in0=lam_t[:, 0:64], in1=lam_t[:, 64:128], op=ALU.mult),
         reads=["lam_t"], writes=["lamtmp0"])
    P.op("dve", lambda e: e.tensor_tensor(out=lamtmp[:, 64:128], in0=lam_t[:, 128:192], in1=lam_t[:, 192:256], op=ALU.mult),
         reads=["lam_t"], writes=["lamtmp1"])
    P.op("dve", lambda e: e.reduce_sum(out=lamred[:, 0:1], in_=lamtmp[:, 0:64], axis=mybir.AxisListType.X),
         reads=["lamtmp0"], writes=["lamred0"])
    P.op("dve", lambda e: e.reduce_sum(out=lamred[:, 1:2], in_=lamtmp[:, 64:128], axis=mybir.AxisListType.X),
         reads=["lamtmp1"], writes=["lamred1"])
    P.op("act", lambda e: e.activation(out=lamred[:, 2:4], in_=lamred[:, 0:2], func=AF.Exp),
         reads=["lamred0", "lamred1"], writes=["lamexp"])
    P.op("dve", lambda e: e.scalar_tensor_tensor(out=neglam[:, 0:1], in0=lamred[:, 3:4], scalar=-LAM_INIT,
                                                 in1=lamred[:, 2:3], op0=ALU.add, op1=ALU.subtract),
         reads=["lamexp"], writes=["neglam"])
    P.op("dve", lambda e: e.tensor_scalar(out=gsub8[:, 0:1], in0=g_sub[:, 0:1], scalar1=(1.0 - LAM_INIT), scalar2=None,
                                          op0=ALU.mult), reads=["g_sub"], writes=["gsub8"])

    base_mark = AR.mark()

    Qd = v3(AR.alloc(8 * NQ * 2, BF), 8)
    P.op("pool", lambda e: e.memset(Qd[:, :, :], 0.0), writes=["Q"])
    Qs = v3(AR.alloc(4 * NQ * 2, BF), 4)
    p1_mark = AR.mark()
    wq = v3(AR.alloc(8 * 1024 * 2, BF), 8)
    wk = v3(AR.alloc(8 * 1024 * 2, BF), 8)
    wv = v3(AR.alloc(8 * 1024 * 2, BF), 8)
    wst = [v3(AR.alloc(8 * 256 * 4), 8) for _ in range(2)]

    wqkv_v = w_qkv.rearrange("(c p) n -> p c n", p=128)
    wplan = [(wq, 0, 0, 0.125), (wk, 0, 512, 1.0), (wv, 0, 1024, 1.0), (wq, 512, 1536, 0.125), (wk, 512, 2048, 1.0), (wv, 512, 2560, 1.0)]
    li = 0
    for (wdst, dcol, scol, scl) in wplan:
        for half in range(2):
            buf = wst[li % 2]
            c0 = scol + half * 256
            P.dma("sp", "wst", 2, lambda e, buf=buf, c0=c0: e.dma_start(out=buf[:, :, :], in_=wqkv_v[:, :, c0:c0 + 256]),
                  writes=[("wst", li % 2)])
            for fc in range(8):
                P.op("dve", lambda e, buf=buf, fc=fc, wdst=wdst, d0=dcol + half * 256, scl=scl: e.tensor_scalar(
                    out=wdst[:, fc, d0:d0 + 256], in0=buf[:, fc, :], scalar1=g_pre[:, fc:fc + 1], scalar2=scl, op0=ALU.mult, op1=ALU.mult),
                    reads=[("wst", li % 2), "g_pre"], writes=["wproj"])
            li += 1

    xt = [alloc_p1(1024 * 4) for _ in range(3)]
    xs = [alloc_p1(1024 * 2, BF) for _ in range(4)]
    junk = alloc_p1(1024 * 2, BF)
    ssq = AR.alloc(64 * 4)
    hT = [v3(AR.alloc(8 * 512 * 2, BF), 8) for _ in range(2)]
    kst = [v3(AR.alloc(8 * 512 * 2, BF), 8) for _ in range(2)]
    vstd = [alloc_p1(4 * 4 * 128 * 2, BF) for _ in range(2)]
    vsts = [AR.alloc(8 * 4 * 128 * 2, BF) for _ in range(2)]
    for i in range(2):
        P.op("pool", lambda e, i=i: e.memset(vsts[i][:, :], 0.0), writes=[("vsts", i)])

    cnt = {"x": 0, "h": 0, "ev": 0}

    NXS = 4

    def x_prep(src_rows_ap, n):
        i = cnt["x"]
        cnt["x"] += 1
        xb = xt[i % 3]
        xsb = xs[i % NXS]
        sc = ssq[:, (i % 16) * 4:(i % 16) * 4 + 4]
        P.dma("sp", "xt", 3, lambda e: e.dma_start(out=xb[0:n, :], in_=src_rows_ap), writes=[("xt", i % 3)])
        P.op("act", lambda e: e.activation(out=junk[0:n, :], in_=xb[0:n, :], func=AF.Square, accum_out=sc[0:n, 0:1]),
             reads=[("xt", i % 3)], writes=["junk", ("ss", i % 16)])
        P.op("act", lambda e: e.activation(out=sc[0:n, 1:2], in_=sc[0:n, 0:1], func=AF.Ln, scale=1.0 / D, bias=eps_t[0:n, 0:1]),
             reads=[("ss", i % 16), "eps"], writes=[("sd", i % 16)])
        P.op("act", lambda e: e.activation(out=sc[0:n, 2:3], in_=sc[0:n, 1:2], func=AF.Exp, scale=-0.5),
             reads=[("sd", i % 16)], writes=[("rs", i % 16)])
        P.op("act", lambda e: e.activation(out=xsb[0:n, :], in_=xb[0:n, :], func=AF.Copy, scale=sc[0:n, 2:3]),
             reads=[("xt", i % 3), ("rs", i % 16)], writes=[("xs", i % NXS)])
        return i

    def x_trans(i, n, hbuf_i, col0):
        xsb = xs[i % NXS]
        pb = 6 + (i % 2)
        tp = v3(PSB[pb], 8)
        for f in range(8):
            P.op("pe", lambda e, f=f: e.transpose(out=tp[:, f, 0:n], in_=xsb[0:n, f * 128:(f + 1) * 128], identity=IDENT[0:n, 0:n]),
                 reads=[("xs", i % NXS), "cm"], writes=[("ps", pb)])
        P.op("dve", lambda e: e.tensor_copy(out=hT[hbuf_i][:, :, col0:col0 + n], in_=tp[:, :, 0:n]),
             reads=[("ps", pb)], writes=[("hT", hbuf_i)])

    def x_to_hT(src_rows_ap, n, hbuf_i, col0):
        x_trans(x_prep(src_rows_ap, n), n, hbuf_i, col0)

    def evac(dst, src, reads, writes):
        i = cnt["ev"]
        cnt["ev"] += 1
        P.op("dve", lambda e: e.tensor_copy(out=dst, in_=src), reads=reads, writes=writes)

    def q_slot(m, hi):
        r0 = 0
        while r0 < TC:
            n = min(128, TC - r0)
            x_to_hT(xq[m * TC + r0:m * TC + r0 + n, :], n, hi, r0)
            r0 += n
        for g in range(8):
            pb = g % 4
            for fc in range(8):
                P.op("pe", lambda e, fc=fc, g=g, pb=pb: e.matmul(
                    PSF[pb][:, 0:TC], lhsT=wq[:, fc, g * 128:(g + 1) * 128], rhs=hT[hi][:, fc, 0:TC],
                    start=(fc == 0), stop=(fc == 7)),
                    reads=[("hT", hi), "wproj"], writes=[("ps", pb)])
            if g < 4:
                evac(Qd[0:64, 2 * g, m * TC:(m + 1) * TC], PSF[pb][0:64, 0:TC], [("ps", pb)], ["Q"])
                evac(Qd[64:128, 2 * g + 1, m * TC:(m + 1) * TC], PSF[pb][64:128, 0:TC], [("ps", pb)], ["Q"])
            else:
                evac(Qs[:, g - 4, m * TC:(m + 1) * TC], PSF[pb][:, 0:TC], [("ps", pb)], ["Q"])

    for m in range(NSLOT):
        q_slot(m, cnt["h"] % 2)
        cnt["h"] += 1

    kt_v = kt_scr.rearrange("(c p) t -> p c t", p=128)
    vd_v = vd_scr.rearrange("u p (b d) -> p u b d", d=128)
    vs_v = vs_scr.rearrange("u p (b d) -> p u b d", d=128)
    NSUP = SEQ // 512

    def kv_prep(s):
        return [x_prep(x_rev[s * 512 + r * 128:s * 512 + (r + 1) * 128, :], 128) for r in range(4)]

    def kv_trans(idx, hi):
        for r in range(4):
            x_trans(idx[r], 128, hi, r * 128)

    def kv_super(s, hi):
        ks = kst[s % 2]
        for dc in range(8):
            pb = dc % 4
            for fc in range(8):
                P.op("pe", lambda e, fc=fc, dc=dc, pb=pb: e.matmul(
                    PSF[pb][:, 0:512], lhsT=wk[:, fc, dc * 128:(dc + 1) * 128], rhs=hT[hi][:, fc, 0:512],
                    start=(fc == 0), stop=(fc == 7)),
                    reads=[("hT", hi), "wproj"], writes=[("ps", pb)])
            evac(ks[:, dc, :], PSF[pb][:, 0:512], [("ps", pb)], [("kst", s % 2)])
        P.dma("pool", "kst", 2, lambda e: e.dma_start(out=kt_v[:, :, s * 512:(s + 1) * 512], in_=ks[:, :, :]),
              reads=[("kst", s % 2)], writes=["kt_scr"])
        vd = vstd[s % 2].rearrange("p (u b d) -> p u b d", u=4, b=4)
        vs = vsts[s % 2].rearrange("p (u b d) -> p u b d", u=8, b=4)
        for r in range(4):
            for cg in range(2):
                pb = 4 + cg
                for fc in range(8):
                    P.op("pe", lambda e, fc=fc, cg=cg, pb=pb, r=r: e.matmul(
                        PSF[pb][:, 0:512], lhsT=hT[hi][:, fc, r * 128:(r + 1) * 128], rhs=wv[:, fc, cg * 512:(cg + 1) * 512],
                        start=(fc == 0), stop=(fc == 7)),
                        reads=[("hT", hi), "wproj"], writes=[("ps", pb)])
                if cg == 0:
                    evac(vd[:, :, r, :], PSF[pb][:, 0:512].rearrange("p (u d) -> p u d", u=4), [("ps", pb)], [("vstd", s % 2)])
                else:
                    src = PSF[pb][:, 0:512].rearrange("p (u d) -> p u d", u=8)
                    for par in range(2):
                        evac(vs[:, par::2, r, par * 64:par * 64 + 64], src[:, par::2, :], [("ps", pb)], [("vsts", s % 2)])
        P.dma("pool", "vst", 2, lambda e: e.dma_start(out=vd_v[:, :, s * 4:(s + 1) * 4, :], in_=vd),
              reads=[("vstd", s % 2)], writes=["vd_scr"])
        P.dma("pool", "vst2", 2, lambda e: e.dma_start(out=vs_v[:, :, s * 4:(s + 1) * 4, :], in_=vs),
              reads=[("vsts", s % 2)], writes=["vs_scr"])

    wup_v0 = w_up.rearrange("(c p) n -> p c n", p=128)
    wcast = []
    for j in range(NJ):
        for half in range(2):
            dstv = wup_scr[j].rearrange("p (c n) -> p c n", c=8)[:, :, half * 128:(half + 1) * 128]
            srcv = wup_v0[:, :, half * DFF + j * 128:half * DFF + (j + 1) * 128]
            wcast.append(("wc%d" % (len(wcast) % 4), dstv, srcv))
    for jj in range(0, NJ, 2):
        dstv = wd_scr.rearrange("p (c n) -> p c n", c=NJ)[:, jj:jj + 2, :]
        srcv = w_down.rearrange("(c p) n -> p c n", p=128)[:, jj:jj + 2, :]
        wcast.append(("wc%d" % (len(wcast) % 4), dstv, srcv))
    for cc in range(0, 8, 2):
        dstv = wo_scr.rearrange("p (c n) -> p c n", c=8)[:, cc:cc + 2, :]
        srcv = w_o.rearrange("(c p) n -> p c n", p=128)[:, cc:cc + 2, :]
        wcast.append(("wc%d" % (len(wcast) % 4), dstv, srcv))

    def issue_wcast(n):
        for _ in range(n):
            if wcast:
                nm, dstv, srcv = wcast.pop(0)
                P.dma("pool", nm, 1, lambda e, dstv=dstv, srcv=srcv: e.dma_start(out=dstv, in_=srcv), writes=["wscr"])

    per_sup = -(-len(wcast) // NSUP)
    h0 = cnt["h"]
    kv_trans(kv_prep(0), h0 % 2)
    for s_ in range(NSUP):
        nxt = kv_prep(s_ + 1) if s_ + 1 < NSUP else None
        kv_super(s_, (h0 + s_) % 2)
        if nxt is not None:
            kv_trans(nxt, (h0 + s_ + 1) % 2)
        issue_wcast(per_sup)
    issue_wcast(len(wcast))

    P.barrier()

    AR.reset(p1_mark)
    KT = AR.alloc(SEQ * 2, BF)
    Vb = v3(AR.alloc(NBLK * 128 * 2, BF), NBLK)
    posrow = AR.alloc(NQ * 4)
    Eb = [AR.alloc(TC * 4) for _ in range(2)]
    Lb = [AR.alloc(TC * 2, BF) for _ in range(2)]
    Ls = [AR.alloc(TC * 2, BF) for _ in range(2)]
    Ab = [AR.alloc(TC * 2, BF) for _ in range(2)]
    Mk = [AR.alloc(TC * 2, BF) for _ in range(3)]
    NBT = 6
    Bt = [AR.alloc(TC * 4) for _ in range(NBT)]
    Tt = [AR.alloc(TC * 4) for _ in range(2)]
    Pb = [AR.alloc(TC * 2, BF) for _ in range(3)]
    Osb = [AR.alloc(TC * 4) for _ in range(4)]
    fin = [AR.alloc(TC * 4) for _ in range(4)]
    finb = AR.alloc(TC * 2, BF)
    lnt = AR.alloc(TC * 4)
    P.dma("sp", "posrow", 1, lambda e: e.dma_start(out=posrow[:, 0:NQ], in_=posrow_in[:, :]), writes=["posrow"])

    def cols(a, n=TC):
        return a[:, 0:n]

    steps_sb = []
    for m in range(NSLOT):
        for kb in range(cfg.NB[m] - 1, -1, -1):
            steps_sb.append((m, kb, kb == cfg.NB[m] - 1, kb == 0, cfg.sb_masked(m, kb)))

    def sb_head(h):
        hh = h % 2
        rs = slice(hh * 64, hh * 64 + 64)
        P.dma("sp", "kt", 1, lambda e: e.dma_start(out=KT[rs, :], in_=kt_scr[512 + 64 * h:512 + 64 * h + 64, :]),
              reads=["kt_scr"], writes=["KT"])
        P.dma("sp", "vb", 1, lambda e: e.dma_start(out=Vb[:, :, :], in_=vs_scr[h].rearrange("p (b d) -> p b d", d=128)),
              reads=["vs_scr"], writes=["V"])
        n = len(steps_sb)

        def st_M(k):
            m, kb, first, last, masked = steps_sb[k]
            if masked:
                P.op("dve", lambda e: e.tensor_scalar(out=cols(Mk[k % 3]), in0=posrow[:, m * TC:(m + 1) * TC],
                                                      scalar1=kcol[:, kb:kb + 1], scalar2=0.0, op0=ALU.subtract, op1=ALU.min),
                     reads=["posrow", "kcol"], writes=[("Mk", k % 3)])

        def warm(nw):
            for _ in range(nw):
                P.op("pe", lambda e: e.matmul(PSF[7][:, 0:512], lhsT=IDENT, rhs=KT[:, 0:512], start=True, stop=True),
                     reads=[], writes=[("ps", 7)])

        def st_Z(k):
            m, kb, first, last, masked = steps_sb[k]
            pb = k % 4
            P.op("pe", lambda e: e.matmul(PSF[pb][:, 0:TC], lhsT=KT[rs, kb * 128:(kb + 1) * 128], rhs=Qs[rs, h // 2, m * TC:(m + 1) * TC],
                                          start=True, stop=False),
                 reads=["KT", "Q"], writes=[("ps", pb)])
            if masked:
                P.op("pe", lambda e: e.matmul(PSF[pb][:, 0:TC], lhsT=IDENT, rhs=cols(Mk[k % 3]), start=False, stop=False),
                     reads=[("Mk", k % 3), "cm"], writes=[("ps", pb)])
            warm(WARM_SB)

        def st_E(k):
            P.op("act", lambda e: e.activation(out=cols(Eb[k % 2]), in_=PSF[k % 4][:, 0:TC], func=AF.Exp),
                 reads=[("ps", k % 4)], writes=[("E", k % 2)])

        def st_L(k):
            P.op("act", lambda e: e.activation(out=cols(Lb[k % 2]), in_=cols(Eb[k % 2]), func=AF.Ln, bias=one_t[:, 0:1], scale=1.0),
                 reads=[("E", k % 2), "one"], writes=[("L", k % 2)])

        def st_LS(k):
            m, kb, first, last, masked = steps_sb[k]
            if last:
                return
            if first:
                P.op("pool", lambda e: e.tensor_copy(out=cols(Ls[(k + 1) % 2]), in_=cols(Lb[k % 2])),
                     reads=[("L", k % 2)], writes=[("Ls", (k + 1) % 2)])
            else:
                P.op("pool", lambda e: e.tensor_tensor(out=cols(Ls[(k + 1) % 2]), in0=cols(Ls[k % 2]), in1=cols(Lb[k % 2]), op=ALU.add),
                     reads=[("L", k % 2), ("Ls", k % 2)], writes=[("Ls", (k + 1) % 2)])

        def st_ZC(k):
            m, kb, first, last, masked = steps_sb[k]
            pb = k % 4
            P.op("pe", lambda e: e.matmul(PSF[pb][:, 0:TC], lhsT=NEGTRI, rhs=cols(Lb[k % 2]), start=False, stop=first, skip_group_check=True),
                 reads=[("L", k % 2), "cm"], writes=[("ps", pb)])
            if not first:
                P.op("pe", lambda e: e.matmul(PSF[pb][:, 0:TC], lhsT=NEGONES, rhs=cols(Ls[k % 2]), start=False, stop=True, skip_group_check=True),
                     reads=[("Ls", k % 2), "cm"], writes=[("ps", pb)])

        def st_A(k):
            pb = k % 4
            P.op("act", lambda e: e.activation(out=cols(Ab[k % 2]), in_=PSF[pb][:, 0:TC], func=AF.Exp),
                 reads=[("ps", pb)], writes=[("A", k % 2)])

        def st_AV(k):
            m, kb, first, last, masked = steps_sb[k]
            ob = 4 + (m % 2)
            P.op("pe", lambda e: e.matmul(PSF[ob][:, 0:TC], lhsT=Vb[:, kb, :], rhs=cols(Ab[k % 2]), start=first, stop=last),
                 reads=["V", ("A", k % 2)], writes=[("ps", ob)])
            if last:
                finalize_sb(m, ob)

        def finalize_sb(m, ob):
            P.op("dve", lambda e: e.tensor_copy(out=cols(fin[0]), in_=PSF[ob][:, 0:TC]), reads=[("ps", ob)], writes=[("fin", 0)])
            P.op("dve", lambda e: e.tensor_tensor(out=cols(finb), in0=cols(fin[0]), in1=cols(fin[0]), op=ALU.mult),
                 reads=[("fin", 0)], writes=["finb"])
            P.op("pe", lambda e: e.matmul(PSF[6][:, 0:TC], lhsT=ONESBLK, rhs=cols(finb), start=True, stop=True),
                 reads=["finb", "cm"], writes=[("ps", 6)])
            P.op("act", lambda e: e.activation(out=cols(lnt), in_=PSF[6][:, 0:TC], func=AF.Ln, scale=1.0 / HD, bias=eps_t[:, 0:1]),
                 reads=[("ps", 6), "eps"], writes=["lnt"])
            P.op("act", lambda e: e.activation(out=cols(fin[1]), in_=cols(lnt), func=AF.Exp, scale=-0.5),
                 reads=["lnt"], writes=[("fin", 1)])
            P.op("dve", lambda e: e.scalar_tensor_tensor(out=mixT3[rs, 4 + h // 2, m * TC:(m + 1) * TC], in0=fin[0][rs, 0:TC],
                                                         scalar=g_sb[rs, 0:1], in1=fin[1][rs, 0:TC], op0=ALU.mult, op1=ALU.mult),
                 reads=[("fin", 0), ("fin", 1), "g_sb"], writes=["mixT"])

        stages = [(st_L, 2), (st_LS, 2), (st_ZC, 2), (st_AV, 4), (st_M, 0), (st_Z, 0), (st_E, 1), (st_A, 3)]
        for t in range(n + 4):
            for fn, lag in stages:
                k = t - lag
                if 0 <= k < n:
                    fn(k)

    for h_ in range(8):
        sb_head(h_)

    steps_df = []
    for m in range(NSLOT):
        for kb in range(cfg.NB[m]):
            steps_df.append((m, kb, kb == 0, kb == cfg.NB[m] - 1, cfg.df_masked(m, kb)))

    def df_head(h):
        P.dma("sp", "kt", 1, lambda e: e.dma_start(out=KT[:, :], in_=kt_scr[128 * h:128 * h + 128, :]),
              reads=["kt_scr"], writes=["KT"])
        P.dma("sp", "vb", 1, lambda e: e.dma_start(out=Vb[:, :, :], in_=vd_scr[h].rearrange("p (b d) -> p b d", d=128)),
              reads=["vd_scr"], writes=["V"])
        n = len(steps_df)
        n2 = 2 * n

        def sd_B(kk):
            k, mm = kk // 2, kk % 2
            m, kb, first, last, masked = steps_df[k]
            if masked and mm == 0:
                src = bass.AP(tensor=gtab.tensor, offset=h * cfg.TABLEN + cfg.base(m, kb), ap=[[1, 128], [1, TC]])
                P.dma("sp", "bt", NBT, lambda e: e.dma_start(out=cols(Bt[k % NBT]), in_=src), writes=[("Bt", k % NBT)])

        def sd_S(kk):
            k, mm = kk // 2, kk % 2
            m, kb, first, last, masked = steps_df[k]
            pb = kk % 2
            P.op("pe", lambda e: e.matmul(PSF[pb][:, 0:TC], lhsT=KT[:, kb * 128:(kb + 1) * 128],
                                          rhs=Qd[:, 2 * h + mm, m * TC:(m + 1) * TC], start=True, stop=True),
                 reads=["KT", "Q"], writes=[("ps", pb)])

        def sd_T(kk):
            k, mm = kk // 2, kk % 2
            m, kb, first, last, masked = steps_df[k]
            if masked:
                P.op("dve", lambda e: e.tensor_tensor(out=cols(Tt[kk % 2]), in0=PSF[kk % 2][:, 0:TC], in1=cols(Bt[k % NBT]), op=ALU.add),
                     reads=[("ps", kk % 2), ("Bt", k % NBT)], writes=[("Tt", kk % 2)])

        def sd_P(kk):
            k, mm = kk // 2, kk % 2
            m, kb, first, last, masked = steps_df[k]
            if masked:
                P.op("act", lambda e: e.activation(out=cols(Pb[kk % 3]), in_=cols(Tt[kk % 2]), func=AF.Exp),
                     reads=[("Tt", kk % 2)], writes=[("Pb", kk % 3)])
            else:
                P.op("act", lambda e: e.activation(out=cols(Pb[kk % 3]), in_=PSF[kk % 2][:, 0:TC], func=AF.Exp, bias=b31_t[:, h:h + 1], scale=1.0),
                     reads=[("ps", kk % 2), "b31"], writes=[("Pb", kk % 3)])

        def sd_PV(kk):
            k, mm = kk // 2, kk % 2
            m, kb, first, last, masked = steps_df[k]
            ob = 2 + mm
            lb = 4 + mm
            P.op("pe", lambda e: e.matmul(PSF[ob][:, 0:TC], lhsT=Vb[:, kb, :], rhs=cols(Pb[kk % 3]), start=first, stop=last),
                 reads=["V", ("Pb", kk % 3)], writes=[("ps", ob)])
            P.op("pe", lambda e: e.matmul(PSF[lb][:, 0:TC], lhsT=ONES, rhs=cols(Pb[kk % 3]), start=first, stop=last),
                 reads=[("Pb", kk % 3), "cm"], writes=[("ps", lb)])
            for _ in range(WARM_DF):
                P.op("pe", lambda e: e.matmul(PSF[7][:, 0:512], lhsT=IDENT, rhs=KT[:, 0:512], start=True, stop=True),
                     reads=[], writes=[("ps", 7)])
            if last and mm == 1:
                finalize_df(m)

        def finalize_df(m):
            for q in range(2):
                P.op("dve", lambda e, q=q: e.tensor_copy(out=cols(Osb[q]), in_=PSF[2 + q][:, 0:TC]), reads=[("ps", 2 + q)], writes=[("Osb", q)])
                P.op("dve", lambda e, q=q: e.tensor_scalar(out=cols(Osb[2 + q]), in0=PSF[4 + q][:, 0:TC], scalar1=1e-30, scalar2=None, op0=ALU.max),
                     reads=[("ps", 4 + q)], writes=[("Osb", 2 + q)])
            for q in range(2):
                P.op("dve", lambda e, q=q: e.reciprocal(out=cols(fin[2 + q]), in_=cols(Osb[2 + q])), reads=[("Osb", 2 + q)], writes=[("fin", 2 + q)])
                P.op("dve", lambda e, q=q: e.tensor_tensor(out=cols(fin[2 + q]), in0=cols(Osb[q]), in1=cols(fin[2 + q]), op=ALU.mult),
                     reads=[("Osb", q), ("fin", 2 + q)], writes=[("fin", 2 + q)])
            P.op("dve", lambda e: e.scalar_tensor_tensor(out=cols(fin[0]), in0=cols(fin[3]), scalar=neglam[:, 0:1], in1=cols(fin[2]),
                                                         op0=ALU.mult, op1=ALU.add),
                 reads=[("fin", 2), ("fin", 3), "neglam"], writes=[("fin", 0)])
            P.op("dve", lambda e: e.tensor_tensor(out=cols(finb), in0=cols(fin[0]), in1=cols(fin[0]), op=ALU.mult),
                 reads=[("fin", 0)], writes=["finb"])
            P.op("pe", lambda e: e.matmul(PSF[6][:, 0:TC], lhsT=ONES, rhs=cols(finb), start=True, stop=True),
                 reads=["finb", "cm"], writes=[("ps", 6)])
            P.op("act", lambda e: e.activation(out=cols(lnt), in_=PSF[6][:, 0:TC], func=AF.Ln, scale=1.0 / 128.0, bias=eps_t[:, 0:1]),
                 reads=[("ps", 6), "eps"], writes=["lnt"])
            P.op("act", lambda e: e.activation(out=cols(fin[1]), in_=cols(lnt), func=AF.Exp, scale=-0.5),
                 reads=["lnt"], writes=[("fin", 1)])
            P.op("dve", lambda e: e.scalar_tensor_tensor(out=mixT3[:, h, m * TC:(m + 1) * TC], in0=cols(fin[0]),
                                                         scalar=gsub8[:, 0:1], in1=cols(fin[1]), op0=ALU.mult, op1=ALU.mult),
                 reads=[("fin", 0), ("fin", 1), "gsub8"], writes=["mixT"])

        stages = [(sd_PV, 3), (sd_B, 0), (sd_S, 0), (sd_T, 1), (sd_P, 1)]
        for t in range(n2 + 3):
            for fn, lag in stages:
                kk = t - lag
                if 0 <= kk < n2:
                    fn(kk)

    for h_ in range(4):
        df_head(h_)

    P.barrier()

    AR.reset(base_mark)
    wo = v3(AR.alloc(8 * 1024 * 2, BF), 8)
    wd = v3(AR.alloc(NJ * 1024 * 2, BF), NJ)
    NWU = 3
    wu = [v3(AR.alloc(8 * 256 * 2, BF), 8) for _ in range(NWU)]
    gvT = v3(AR.alloc(NJ * TC * 2, BF), NJ)
    xT = v3(AR.alloc(8 * TC * 4), 8)
    Mo = v3(AR.alloc(8 * TC * 4), 8)
    sqb = v3(AR.alloc(8 * TC * 2, BF), 8)
    x1T = v3(AR.alloc(8 * TC * 4), 8)
    h2T = v3(AR.alloc(8 * TC * 2, BF), 8)
    rstd = AR.alloc(TC * 4)
    tmp3 = AR.alloc(TC * 4)
    yg = [AR.alloc(TC * 4) for _ in range(2)]
    yv = [AR.alloc(TC * 4) for _ in range(2)]
    gl = [AR.alloc(TC * 4) for _ in range(2)]
    xrow = [AR.alloc(1024 * 4) for _ in range(2)]

    P.dma("sp", "wo", 1, lambda e: e.dma_start(out=wo[:, :, :], in_=wo_scr.rearrange("p (c n) -> p c n", c=8)), writes=["wo"])
    P.dma("sp", "wd", 1, lambda e: e.dma_start(out=wd[:, :, :], in_=wd_scr.rearrange("p (c n) -> p c n", c=NJ)), writes=["wd"])
    RB = 1

    def rms_rstd(src3, nfeat, tagname):
        for dc in range(8):
            P.op("pool", lambda e, dc=dc: e.tensor_tensor(out=sqb[:, dc, :], in0=src3[:, dc, :], in1=src3[:, dc, :], op=ALU.mult),
                 reads=[tagname], writes=["sqb"])
        for dc in range(8):
            P.op("pe", lambda e, dc=dc: e.matmul(PSF[RB][:, 0:TC], lhsT=ONES, rhs=sqb[:, dc, :], start=(dc == 0), stop=(dc == 7)),
                 reads=["sqb", "cm"], writes=[("ps", RB)])
        P.op("act", lambda e: e.activation(out=cols(tmp3), in_=PSF[RB][:, 0:TC], func=AF.Ln, scale=1.0 / nfeat, bias=eps_t[:, 0:1]),
             reads=[("ps", RB), "eps"], writes=["tmp3"])
        P.op("act", lambda e: e.activation(out=cols(rstd), in_=cols(tmp3), func=AF.Exp, scale=-0.5), reads=["tmp3"], writes=["rstd"])

    jcount = {"n": 0, "row": 0}

    def ffn_j(m, j):
        ji = jcount["n"]
        jcount["n"] += 1
        wb = wu[ji % NWU]
        P.dma("sp", "wu", NWU, lambda e: e.dma_start(out=wb[:, :, :], in_=wup_scr[j].rearrange("p (c n) -> p c n", c=8)),
              writes=[("wu", ji % NWU)])
        for half in range(2):
            pb = 4 + 2 * (ji % 2) + half
            for fc in range(8):
                P.op("pe", lambda e, fc=fc, half=half, pb=pb: e.matmul(
                    PSF[pb][:, 0:TC], lhsT=wb[:, fc, half * 128:(half + 1) * 128], rhs=h2T[:, fc, :], start=(fc == 0), stop=(fc == 7)),
                    reads=[("wu", ji % NWU), "h2T"], writes=[("ps", pb)])
            ydst = (yg if half == 0 else yv)[ji % 2]
            ch = j + half * NJ
            yname = ("y", half, ji % 2)
            P.op("dve", lambda e, ydst=ydst, pb=pb, ch=ch: e.tensor_scalar(
                out=cols(ydst), in0=PSF[pb][:, 0:TC], scalar1=cw[:, ch * 3 + 2:ch * 3 + 3], scalar2=cb[:, ch:ch + 1], op0=ALU.mult, op1=ALU.add),
                reads=[("ps", pb), "cw", "cb"], writes=[yname])
            P.op("dve", lambda e, ydst=ydst, pb=pb, ch=ch: e.scalar_tensor_tensor(
                out=ydst[:, 1:TC], in0=PSF[pb][:, 0:TC - 1], scalar=cw[:, ch * 3 + 1:ch * 3 + 2], in1=ydst[:, 1:TC], op0=ALU.mult, op1=ALU.add),
                reads=[("ps", pb), "cw", yname], writes=[yname])
            P.op("dve", lambda e, ydst=ydst, pb=pb, ch=ch: e.scalar_tensor_tensor(
                out=ydst[:, 2:TC], in0=PSF[pb][:, 0:TC - 2], scalar=cw[:, ch * 3 + 0:ch * 3 + 1], in1=ydst[:, 2:TC], op0=ALU.mult, op1=ALU.add),
                reads=[("ps", pb), "cw", yname], writes=[yname])
        P.op("act", lambda e: e.activation(out=cols(gl[ji % 2]), in_=cols(yg[ji % 2]), func=AF.Gelu_apprx_tanh),
             reads=[("y", 0, ji % 2)], writes=[("gl", ji % 2)])
        P.op("pool", lambda e: e.tensor_tensor(out=gvT[:, j, :], in0=cols(gl[ji % 2]), in1=cols(yv[ji % 2]), op=ALU.mult),
             reads=[("gl", ji % 2), ("y", 1, ji % 2)], writes=["gvT"])

    def p3_slot(m):
        r0 = 0
        while r0 < TC:
            n = min(128, TC - r0)
            ri = jcount["row"]
            jcount["row"] += 1
            xb = xrow[ri % 2]
            P.dma("sp", "xrow", 2, lambda e, xb=xb, n=n, r0=r0: e.dma_start(out=xb[0:n, :], in_=xq[m * TC + r0:m * TC + r0 + n, :]),
                  writes=[("xrow", ri % 2)])
            for half in range(2):
                pb = 2 + half
                for f4 in range(4):
                    f = half * 4 + f4
                    P.op("pe", lambda e, xb=xb, n=n, f=f, f4=f4, pb=pb: e.transpose(
                        out=PSF[pb][:, f4 * 128:f4 * 128 + n], in_=xb[0:n, f * 128:(f + 1) * 128], identity=IDENTF[0:n, 0:n]),
                        reads=[("xrow", ri % 2), "identf"], writes=[("ps", pb)])
                P.op("dve", lambda e, n=n, r0=r0, half=half, pb=pb: e.tensor_copy(
                    out=xT[:, half * 4:half * 4 + 4, r0:r0 + n], in_=PSF[pb][:, :].rearrange("p (a b) -> p a b", a=4)[:, :, 0:n]),
                    reads=[("ps", pb)], writes=["xT"])
            r0 += n
        for dc in range(8):
            pb = 2 + dc % 2
            for fc in range(8):
                P.op("pe", lambda e, dc=dc, fc=fc, pb=pb: e.matmul(PSF[pb][:, 0:TC], lhsT=wo[:, fc, dc * 128:(dc + 1) * 128],
                                                                   rhs=mixT3[:, fc, m * TC:(m + 1) * TC], start=(fc == 0), stop=(fc == 7)),
                     reads=["wo", "mixT"], writes=[("ps", pb)])
            P.op("dve", lambda e, dc=dc, pb=pb: e.tensor_copy(out=Mo[:, dc, :], in_=PSF[pb][:, 0:TC]), reads=[("ps", pb)], writes=["Mo"])
        rms_rstd(Mo, D, "Mo")
        for dc in range(8):
            P.op("dve", lambda e, dc=dc: e.scalar_tensor_tensor(out=Mo[:, dc, :], in0=Mo[:, dc, :], scalar=g_post[:, dc:dc + 1], in1=cols(rstd),
                                                                op0=ALU.mult, op1=ALU.mult), reads=["Mo", "rstd", "g_post"], writes=["Mo"])
            P.op("dve", lambda e, dc=dc: e.tensor_tensor(out=x1T[:, dc, :], in0=Mo[:, dc, :], in1=xT[:, dc, :], op=ALU.add),
                 reads=["Mo", "xT"], writes=["x1T"])
        rms_rstd(x1T, D, "x1T")
        for dc in range(8):
            P.op("dve", lambda e, dc=dc: e.scalar_tensor_tensor(out=h2T[:, dc, :], in0=x1T[:, dc, :], scalar=g_ffn[:, dc:dc + 1], in1=cols(rstd),
                                                                op0=ALU.mult, op1=ALU.mult), reads=["x1T", "rstd", "g_ffn"], writes=["h2T"])
        P.op("dve", lambda e: e.tensor_scalar(out=h2T[:, :, 0:2], in0=h2T[:, :, 0:2], scalar1=hv[:, m:m + 1], scalar2=None, op0=ALU.mult),
             reads=["h2T", "hv"], writes=["h2T"])
        for j in range(NJ):
            ffn_j(m, j)
        for dc in range(8):
            pb = 2 + dc % 2
            for j in range(NJ):
                P.op("pe", lambda e, dc=dc, j=j, pb=pb: e.matmul(PSF[pb][:, 0:TC], lhsT=wd[:, j, dc * 128:(dc + 1) * 128], rhs=gvT[:, j, :],
                                                                 start=(j == 0), stop=(j == NJ - 1)),
                     reads=["wd", "gvT"], writes=[("ps", pb)])
            P.op("dve", lambda e, dc=dc, pb=pb: e.tensor_copy(out=Mo[:, dc, :], in_=PSF[pb][:, 0:TC]), reads=[("ps", pb)], writes=["Mo"])
        rms_rstd(Mo, D, "Mo")
        for dc in range(8):
            P.op("dve", lambda e, dc=dc: e.scalar_tensor_tensor(out=Mo[:, dc, :], in0=Mo[:, dc, :], scalar=g_post2[:, dc:dc + 1], in1=cols(rstd),
                                                                op0=ALU.mult, op1=ALU.mult), reads=["Mo", "rstd", "g_post2"], writes=["Mo"])
            P.op("dve", lambda e, dc=dc: e.tensor_tensor(out=xT[:, dc, :], in0=Mo[:, dc, :], in1=x1T[:, dc, :], op=ALU.add),
                 reads=["Mo", "x1T"], writes=["xT"])
        r0 = 0
        while r0 < TC:
            n = min(128, TC - r0)
            ri = jcount["row"]
            jcount["row"] += 1
            ob_ = xrow[ri % 2]
            for half in range(2):
                pb = 2 + half
                for f4 in range(4):
                    f = half * 4 + f4
                    P.op("pe", lambda e, n=n, r0=r0, f=f, f4=f4, pb=pb: e.transpose(
                        out=PSF[pb][0:n, f4 * 128:(f4 + 1) * 128], in_=xT[:, f, r0:r0 + n], identity=IDENTF[:, :]),
                        reads=["xT", "identf"], writes=[("ps", pb)])
                P.op("dve", lambda e, n=n, half=half, pb=pb, ob_=ob_: e.tensor_copy(out=ob_[0:n, half * 512:(half + 1) * 512], in_=PSF[pb][0:n, 0:512]),
                     reads=[("ps", pb)], writes=[("xrow", ri % 2)])
            o = P.dma("sp", "xrow", 2, lambda e, n=n, r0=r0, ob_=ob_: e.dma_start(out=out[m * TC + r0:m * TC + r0 + n, :], in_=ob_[0:n, :]),
                      reads=[("xrow", ri % 2)], writes=["out"])
            P.out_dmas.append(o)
            r0 += n

    for m_ in range(NSLOT):
        p3_slot(m_)

    if debug:
        o = P.dma("sp", "dbg", 1, lambda e: e.dma_start(out=dbg[:, :], in_=mixT[:, :]), reads=["mixT"], writes=["dbg"])
        P.out_dmas.append(o)

    P.emit()
    st.close()
    return nc


def _bucket(n):
    n = np.maximum(n, 0)
    nf = np.maximum(n, 1).astype(np.float32)
    large = 16 + (np.log(nf / np.float32(16)) / np.float32(math.log(128 / 16)) * np.float32(16)).astype(np.int32)
    large = np.minimum(large, 31)
    return np.where(n < 16, n, large)


def make_inputs(cfg, inp):
    SEQ, NSLOT, W, TC, NBLK, NQ = cfg.SEQ, cfg.NSLOT, cfg.W, cfg.TC, cfg.NBLK, cfg.NQ
    f32 = np.float32
    x = np.ascontiguousarray(np.asarray(inp["x"], f32)[0])
    x_rev = np.ascontiguousarray(x.reshape(NBLK, 128, D)[:, ::-1, :].reshape(SEQ, D))

    def pc(v, c):
        return np.ascontiguousarray(np.asarray(v, f32).reshape(c, 128).T)

    shared = {
        "x_rev": x_rev,
        "w_qkv": np.ascontiguousarray(np.asarray(inp["w_qkv"], f32)[0]),
        "w_o": np.ascontiguousarray(np.asarray(inp["w_o"], f32)[0]),
        "w_up": np.ascontiguousarray(np.asarray(inp["w_up"], f32)[0]),
        "w_down": np.ascontiguousarray(np.asarray(inp["w_down"], f32)[0]),
        "gpre": pc(inp["attn_pre_norm"][0], 8),
        "gpost": pc(inp["attn_post_norm"][0], 8),
        "gffn": pc(inp["ffn_pre_norm"][0], 8),
        "gpost2": pc(inp["ffn_post_norm"][0], 8),
        "cw": np.ascontiguousarray(np.asarray(inp["conv_w"], f32)[0].reshape(3, 44, 128).transpose(2, 1, 0).reshape(128, 132)),
        "cb": pc(inp["conv_b"][0], 44),
        "lamv": np.ascontiguousarray(np.broadcast_to(np.concatenate(
            [np.asarray(inp[k], f32)[0] for k in ("lambda_q1", "lambda_k1", "lambda_q2", "lambda_k2")])[None, :], (128, 256))),
        "gsub": np.ascontiguousarray(np.asarray(inp["diff_subln"], f32)[0].reshape(128, 1)),
        "gsb": np.ascontiguousarray(np.tile(np.asarray(inp["sb_norm"], f32)[0], 2).reshape(128, 1)),
        "b31": np.ascontiguousarray(np.broadcast_to(np.asarray(inp["rel_bias"], f32)[31][None, :], (128, 4))),
        "identf": np.eye(128, dtype=f32),
    }
    pj = np.arange(128)
    ident = np.eye(128, dtype=f32)
    negtri = -(pj[:, None] <= pj[None, :]).astype(f32)
    negones = -np.ones((128, 128), f32)
    ones = np.ones((128, 128), f32)
    onesblk = ((pj[:, None] // 64) == (pj[None, :] // 64)).astype(f32)
    shared["cmats"] = np.concatenate([ident, negtri, negones, ones, onesblk], axis=1).astype(ml_dtypes.bfloat16)
    kb = np.arange(NBLK)
    shared["kcol"] = np.ascontiguousarray(((128 * kb[None, :] + 127 - pj[:, None]) * BIG).astype(f32))

    rel_bias = np.asarray(inp["rel_bias"], f32)
    maps = []
    for c in range(NCORES):
        mp = dict(shared)
        idx = np.zeros(NQ, np.int64)
        pos = np.zeros(NQ, np.int64)
        hvv = np.ones((128, NSLOT), f32)
        for m in range(NSLOT):
            t = cfg.t0(c, m) + np.arange(TC)
            pos[m * TC:(m + 1) * TC] = t
            idx[m * TC:(m + 1) * TC] = np.clip(t, 0, SEQ - 1)
            if t[0] < 0:
                hvv[:, m] = 0.0
        mp["xq"] = np.ascontiguousarray(x[idx])
        mp["hv"] = hvv
        mp["posrow"] = np.ascontiguousarray(np.broadcast_to(((pos - 1) * BIG).astype(f32)[None, :], (128, NQ)))
        n = np.arange(cfg.TABLEN)
        d = n - cfg.OFF0 + W * c - 2 - 127
        tab = np.where(d[None, :] >= 0, rel_bias[_bucket(d), :].T, f32(NEG)).astype(f32)
        mp["gtab"] = np.ascontiguousarray(tab)
        maps.append(mp)
    return maps


def assemble(cfg, results):
    SEQ, NSLOT, W, TC = cfg.SEQ, cfg.NSLOT, cfg.W, cfg.TC
    outp = np.zeros((SEQ, D), np.float32)
    for c in range(NCORES):
        o = results[c]["out"]
        for m in range(NSLOT):
            t0 = W * (8 * m + c)
            n = min(W, SEQ - t0)
            if n <= 0:
                continue
            outp[t0:t0 + n] = o[m * TC + 2:m * TC + 2 + n]
    return outp[None]


_NC_CACHE = {}


def run(cfg, inp, debug=None, trace=False):
    key = (cfg.SEQ, cfg.NSLOT, cfg.W, debug)
    if key not in _NC_CACHE:
        _NC_CACHE[key] = build(cfg, debug)
    nc = _NC_CACHE[key]
    maps = make_inputs(cfg, inp)
    res = run_bass_kernel_spmd(nc, maps, core_ids=list(range(NCORES)), **({"trace": True} if trace else {}))
    return res


def kernel(**inputs):
    res = run(FULL, inputs)
    return assemble(FULL, res.results)
```
